# Optimizing a Trainium2 kernel written in Bass

```python
import jax, jax.numpy as jnp
from jax import lax
import numpy as np

D_MODEL = 1024
BATCH = 16
SEQ = 2048
DEPTH = 4
DEC_BATCH = 8
DEC_SEQ = 64
PAST_LEN = 2048

CHUNK = 64
N_MEM = 256
HG_DK = 128
HG_WIDTH = D_MODEL // 2
HG_HEADS = HG_WIDTH // HG_DK
CONV_WIDTH = D_MODEL // 4
CONV_K = 31
POOL_WIDTH = D_MODEL // 4
POOL_WINDOWS = (2, 4, 8, 16)
POOL_GROUPS = len(POOL_WINDOWS)
POOL_GDIM = POOL_WIDTH // POOL_GROUPS
POOL_HIST = max(POOL_WINDOWS) - 1
D_MIX = HG_WIDTH + CONV_WIDTH + POOL_WIDTH
D_IN = 4 * HG_WIDTH + 2 * CONV_WIDTH + POOL_WIDTH
IN_SPLITS = (HG_WIDTH, 2 * HG_WIDTH, 3 * HG_WIDTH, 4 * HG_WIDTH,
             4 * HG_WIDTH + CONV_WIDTH, 4 * HG_WIDTH + 2 * CONV_WIDTH)
X_HEADS = 4
X_HDIM = D_MODEL // X_HEADS
D_FF = 4 * D_MODEL
EPS = 1e-6
F_FLOOR = 1e-30

kernel_name = "hymba_streaming_hgrn2_conformer_pool"


def rms_norm(x, g):
    xf = x.astype(jnp.float32)
    y = xf * lax.rsqrt(jnp.mean(xf * xf, axis=-1, keepdims=True) + EPS)
    return (y * g.astype(jnp.float32)).astype(x.dtype)


def layer_norm(x, g, b):
    xf = x.astype(jnp.float32)
    xc = xf - jnp.mean(xf, axis=-1, keepdims=True)
    y = xc * lax.rsqrt(jnp.mean(xc * xc, axis=-1, keepdims=True) + EPS)
    return (y * g.astype(jnp.float32) + b.astype(jnp.float32)).astype(x.dtype)


def hgrn2_block(S0, q, k, v, log_f):
    T = q.shape[2]
    A = jnp.cumsum(log_f, axis=2)
    causal = jnp.tril(jnp.ones((T, T), dtype=bool))[:, :, None]
    diff = A[:, :, :, None, :] - A[:, :, None, :, :]
    decay = jnp.where(causal, jnp.exp(jnp.where(causal, diff, 0.0)), 0.0)
    scores = jnp.einsum("bhtd,bhsd,bhtsd->bhts", q, k, decay)
    o = (jnp.einsum("bhts,bhsv->bhtv", scores, v)
         + jnp.einsum("bhtd,bhdv->bhtv", q * jnp.exp(A), S0))
    A_end = A[:, :, -1:, :]
    S_new = (jnp.exp(A_end[:, :, 0, :])[..., None] * S0
             + jnp.einsum("bhsd,bhsv->bhdv", k * jnp.exp(A_end - A), v))
    return S_new, o


def hgrn2_recurrence(S0, q, k, v, log_f):
    B, H, T, _ = q.shape
    if T <= CHUNK:
        return hgrn2_block(S0, q, k, v, log_f)
    nc = T // CHUNK

    def to_blocks(t):
        return jnp.moveaxis(t.reshape(B, H, nc, CHUNK, t.shape[-1]), 2, 0)

    S, o = lax.scan(lambda s, xs: hgrn2_block(s, *xs), S0,
                    (to_blocks(q), to_blocks(k), to_blocks(v), to_blocks(log_f)))
    o = jnp.moveaxis(o, 0, 2).reshape(B, H, T, o.shape[-1])
    return S, o


def token_mixers(h, hg_state, conv_buf, pool_buf, pos0, lb, w_in, hg_norm_g,
                 conv_w, conv_b, conv_ln_g, conv_ln_b, pool_w, pool_scale):
    B, T, _ = h.shape
    f32 = jnp.float32
    z = h @ w_in
    q, fl, iv, og, ca, cg, pu = jnp.split(z, IN_SPLITS, axis=-1)

    def heads(t):
        return t.reshape(B, T, HG_HEADS, HG_DK).transpose(0, 2, 1, 3).astype(f32)
    qh = heads(jax.nn.silu(q))
    fh = heads(fl)
    vh = heads(iv)
    lbh = lb.astype(f32).reshape(HG_HEADS, 1, HG_DK)
    f_gate = lbh + (1.0 - lbh) * jax.nn.sigmoid(fh)
    log_f = jnp.log(jnp.maximum(f_gate, F_FLOOR))
    kh = (1.0 - lbh) * jax.nn.sigmoid(-fh)
    S_new, o = hgrn2_recurrence(hg_state.astype(f32), qh, kh, vh, log_f)
    o = o.transpose(0, 2, 1, 3)
    o = rms_norm(o, hg_norm_g.reshape(HG_HEADS, HG_DK)).reshape(B, T, HG_WIDTH)
    a_out = (o * jax.nn.silu(og.astype(f32))).astype(h.dtype)

    glu = ca * jax.nn.sigmoid(cg)
    full_c = jnp.concatenate([conv_buf.astype(glu.dtype), glu], axis=1)
    dw = lax.conv_general_dilated(
        full_c, conv_w[:, None, :].astype(full_c.dtype), window_strides=(1,),
        padding="VALID", dimension_numbers=("NWC", "WIO", "NWC"),
        feature_group_count=CONV_WIDTH) + conv_b.astype(full_c.dtype)
    b_out = jax.nn.silu(layer_norm(dw, conv_ln_g, conv_ln_b)).astype(h.dtype)
    new_conv_buf = full_c[:, -(CONV_K - 1):]

    full_p = jnp.concatenate([pool_buf.astype(pu.dtype), pu], axis=1)
    cs = jnp.cumsum(full_p.astype(f32), axis=1)
    cs = jnp.pad(cs, ((0, 0), (1, 0), (0, 0)))
    pos = pos0 + jnp.arange(T)
    diffs = []
    for gi, w in enumerate(POOL_WINDOWS):
        lo, hi = gi * POOL_GDIM, (gi + 1) * POOL_GDIM
        win_sum = cs[:, POOL_HIST + 1:, lo:hi] - cs[:, POOL_HIST + 1 - w:POOL_HIST + 1 - w + T, lo:hi]
        cnt = jnp.minimum(pos + 1, w).astype(f32)[None, :, None]
        diffs.append(win_sum / cnt - pu[:, :, lo:hi].astype(f32))
    dpool = jnp.stack(diffs, axis=2)
    c_out = (jnp.einsum("btgc,gcd->btgd", dpool, pool_w.astype(f32)).reshape(B, T, POOL_WIDTH)
             * pool_scale.astype(f32)).astype(h.dtype)
    new_pool_buf = full_p[:, -POOL_HIST:]

    mix = jnp.concatenate([a_out, b_out, c_out], axis=-1)
    return mix, S_new.astype(hg_state.dtype), new_conv_buf, new_pool_buf


def cross_attention(h, mk, mv, wq, wo):
    B, T, _ = h.shape
    q = (h @ wq).reshape(B, T, X_HEADS, X_HDIM)
    s = jnp.einsum("bthd,bmhd->bhtm", q, mk).astype(jnp.float32) / np.float32(np.sqrt(X_HDIM))
    p = jax.nn.softmax(s, axis=-1).astype(h.dtype)
    o = jnp.einsum("bhtm,bmhd->bthd", p, mv).reshape(B, T, D_MODEL)
    return o @ wo


def run_trunk(x, mem_k, mem_v, hg_state, conv_buf, pool_buf, pos0, lb_all, p):
    new_hg, new_conv, new_pool = [], [], []
    for li in range(DEPTH):
        h = rms_norm(x, p["norm_mix_g"][li])
        mix, s_hg, s_conv, s_pool = token_mixers(
            h, hg_state[li], conv_buf[li], pool_buf[li], pos0, lb_all[li],
            p["w_in"][li], p["hg_norm_g"][li], p["conv_w"][li], p["conv_b"][li],
            p["conv_ln_g"][li], p["conv_ln_b"][li], p["pool_w"][li], p["pool_scale"][li])
        x = x + mix @ p["w_out"][li]
        x = x + cross_attention(rms_norm(x, p["norm_x_g"][li]), mem_k[li], mem_v[li],
                                p["xq_w"][li], p["xo_w"][li])
        h = rms_norm(x, p["norm_ffn_g"][li])
        x = x + jnp.square(jax.nn.relu(h @ p["w_up"][li])) @ p["w_down"][li]
        new_hg.append(s_hg)
        new_conv.append(s_conv)
        new_pool.append(s_pool)
    return rms_norm(x, p["final_g"]), jnp.stack(new_hg), jnp.stack(new_conv), jnp.stack(new_pool)


def setup_inputs(seed: int = 0) -> dict:
    key = jax.random.key(seed)
    ks = jax.random.split(key, 32)
    f32 = jnp.float32

    def nrm(k, shape, scale):
        return jax.random.normal(k, shape, f32) * scale

    def gain(k, shape):
        return 1.0 + 0.05 * jax.random.normal(k, shape, f32)

    return {
        "x_prompt": nrm(ks[0], (BATCH, SEQ, D_MODEL), 1.0),
        "x_sample": nrm(ks[1], (DEC_BATCH, DEC_SEQ, D_MODEL), 1.0),
        "mem_prompt": nrm(ks[2], (BATCH, N_MEM, D_MODEL), 1.0),
        "state_hgrn": nrm(ks[3], (DEPTH, DEC_BATCH, HG_HEADS, HG_DK, HG_DK), 0.5),
        "cache_conv": nrm(ks[4], (DEPTH, DEC_BATCH, CONV_K - 1, CONV_WIDTH), 0.5),
        "cache_pool": nrm(ks[5], (DEPTH, DEC_BATCH, POOL_HIST, POOL_WIDTH), 1.0),
        "cache_mem_k": nrm(ks[6], (DEPTH, DEC_BATCH, N_MEM, X_HEADS, X_HDIM), 1.0),
        "cache_mem_v": nrm(ks[7], (DEPTH, DEC_BATCH, N_MEM, X_HEADS, X_HDIM), 1.0),
        "norm_mix_g": gain(ks[8], (DEPTH, D_MODEL)),
        "w_in": nrm(ks[9], (DEPTH, D_MODEL, D_IN), D_MODEL ** -0.5),
        "lb_param": nrm(ks[10], (DEPTH, HG_WIDTH), 0.1),
        "hg_norm_g": gain(ks[11], (DEPTH, HG_WIDTH)),
        "conv_w": nrm(ks[12], (DEPTH, CONV_K, CONV_WIDTH), CONV_K ** -0.5),
        "conv_b": nrm(ks[13], (DEPTH, CONV_WIDTH), 0.02),
        "conv_ln_g": gain(ks[14], (DEPTH, CONV_WIDTH)),
        "conv_ln_b": nrm(ks[15], (DEPTH, CONV_WIDTH), 0.02),
        "pool_w": nrm(ks[16], (DEPTH, POOL_GROUPS, POOL_GDIM, POOL_GDIM), POOL_GDIM ** -0.5),
        "pool_scale": gain(ks[17], (DEPTH, POOL_WIDTH)),
        "w_out": nrm(ks[18], (DEPTH, D_MIX, D_MODEL), D_MIX ** -0.5),
        "norm_x_g": gain(ks[19], (DEPTH, D_MODEL)),
        "norm_mem_g": gain(ks[20], (DEPTH, D_MODEL)),
        "xq_w": nrm(ks[21], (DEPTH, D_MODEL, D_MODEL), D_MODEL ** -0.5),
        "xk_w": nrm(ks[22], (DEPTH, D_MODEL, D_MODEL), D_MODEL ** -0.5),
        "xv_w": nrm(ks[23], (DEPTH, D_MODEL, D_MODEL), D_MODEL ** -0.5),
        "xo_w": nrm(ks[24], (DEPTH, D_MODEL, D_MODEL), D_MODEL ** -0.5),
        "norm_ffn_g": gain(ks[25], (DEPTH, D_MODEL)),
        "w_up": nrm(ks[26], (DEPTH, D_MODEL, D_FF), D_MODEL ** -0.5),
        "w_down": nrm(ks[27], (DEPTH, D_FF, D_MODEL), D_FF ** -0.5),
        "final_g": gain(ks[28], (D_MODEL,)),
    }


def reference(x_prompt, x_sample, mem_prompt, state_hgrn, cache_conv, cache_pool,
              cache_mem_k, cache_mem_v, norm_mix_g, w_in, lb_param, hg_norm_g,
              conv_w, conv_b, conv_ln_g, conv_ln_b, pool_w, pool_scale, w_out,
              norm_x_g, norm_mem_g, xq_w, xk_w, xv_w, xo_w, norm_ffn_g, w_up,
              w_down, final_g):
    p = dict(norm_mix_g=norm_mix_g, w_in=w_in, hg_norm_g=hg_norm_g, conv_w=conv_w,
             conv_b=conv_b, conv_ln_g=conv_ln_g, conv_ln_b=conv_ln_b, pool_w=pool_w,
             pool_scale=pool_scale, w_out=w_out, norm_x_g=norm_x_g, xq_w=xq_w, xo_w=xo_w,
             norm_ffn_g=norm_ffn_g, w_up=w_up, w_down=w_down, final_g=final_g)

    sm = jax.nn.softmax(lb_param.astype(jnp.float32), axis=0)
    lb_all = jnp.cumsum(sm, axis=0) - sm[0:1]

    Bp = mem_prompt.shape[0]
    mem_k_p = jnp.stack([
        (rms_norm(mem_prompt, norm_mem_g[li]) @ xk_w[li]).reshape(Bp, N_MEM, X_HEADS, X_HDIM)
        for li in range(DEPTH)])
    mem_v_p = jnp.stack([
        (rms_norm(mem_prompt, norm_mem_g[li]) @ xv_w[li]).reshape(Bp, N_MEM, X_HEADS, X_HDIM)
        for li in range(DEPTH)])

    dt = x_prompt.dtype
    hg0 = jnp.zeros((DEPTH, Bp, HG_HEADS, HG_DK, HG_DK), dt)
    conv0 = jnp.zeros((DEPTH, Bp, CONV_K - 1, CONV_WIDTH), dt)
    pool0 = jnp.zeros((DEPTH, Bp, POOL_HIST, POOL_WIDTH), dt)
    y_prompt, hg_p, conv_p, pool_p = run_trunk(
        x_prompt, mem_k_p, mem_v_p, hg0, conv0, pool0, 0, lb_all, p)

    y_sample, hg_s, conv_s, pool_s = run_trunk(
        x_sample, cache_mem_k, cache_mem_v, state_hgrn, cache_conv, cache_pool,
        PAST_LEN, lb_all, p)

    return (y_prompt, y_sample, hg_p, conv_p, pool_p, mem_k_p, mem_v_p, hg_s, conv_s, pool_s)
```

```python
import numpy as np
import ml_dtypes
import concourse.bass as bass
import concourse.mybir as mybir
from concourse.bass_utils import run_bass_kernel_spmd

F32 = mybir.dt.float32
BF = mybir.dt.bfloat16
AF = mybir.ActivationFunctionType
ALU = mybir.AluOpType
AX = mybir.AxisListType

CFG = dict(DEPTH=4, SEQ=2048, NP=2, NS=1, DEC_SEQ=64, NCORES=8, PAST=2048)
D = 1024
NMEM = 256
EPS = 1e-6
TT = 512
NSLOT = 4
PIECES = ["in5", "in4", "in1", "in3", "in0", "in2", "out0", "out1", "q0", "q1", "o0", "o1",
          "u0", "u1", "u2", "u3", "d00", "d10", "d01", "d11",
          "u4", "u5", "u6", "u7", "d20", "d30", "d21", "d31", "k0", "k1", "v0", "v1"]
NPL = len(PIECES)
PIDX = {n: i for i, n in enumerate(PIECES)}


class Buf:
    __slots__ = ("name", "w", "r", "init", "excl")

    def __init__(self, name, init=(), excl=False):
        self.name = name
        self.excl = excl
        self.w = None
        self.r = {}
        self.init = tuple(init)


class Op:
    __slots__ = ("eng", "fn", "deps", "sig", "signo", "dma", "dmaidx", "id", "label")


class Prog:
    def __init__(self):
        self.ops = []
        self.last_dma = {}
        self.dma_count = {}
        self.last_eng = {}
        self.cur = ""

    def add(self, eng, fn, r=(), w=(), dma=None):
        o = Op()
        o.eng, o.fn, o.dma, o.sig, o.signo, o.dmaidx = eng, fn, dma, False, 0, 0
        o.id = len(self.ops)
        o.label = self.cur
        deps = set()
        mykey = eng if dma is None else "dma:" + dma
        for b in r:
            if b.w is not None:
                deps.add(b.w)
            deps.update(b.init)
            if b.excl:
                deps.update(o2 for k2, o2 in b.r.items() if k2 != mykey)
        for b in w:
            if b.w is not None:
                deps.add(b.w)
            deps.update(b.r.values())
            deps.update(b.init)
        if dma is not None:
            prev = self.last_dma.get(dma)
            if prev is not None:
                deps.add(prev)
            self.last_dma[dma] = o
            o.dmaidx = self.dma_count.get(dma, 0) + 1
            self.dma_count[dma] = o.dmaidx
        deps.discard(o)
        if dma is None and eng == "pe":
            deps = {d for d in deps if not (d.dma is None and d.eng == "pe")}
        o.deps = deps
        for d in deps:
            d.sig = True
        key = eng if dma is None else "dma:" + dma
        for b in r:
            b.r[key] = o
        for b in w:
            b.w = o
            b.r = {}
            b.init = ()
        self.ops.append(o)
        if dma is None:
            self.last_eng[eng] = o
        return o

    def fence(self):
        return list(self.last_eng.values()) + list(self.last_dma.values())


def build(cfg):
    DEPTH, SEQ, NP, NS, DSEQ = cfg["DEPTH"], cfg["SEQ"], cfg["NP"], cfg["NS"], cfg["DEC_SEQ"]
    nc = bass.Bass("TRN2", target_bir_lowering=False)
    P = Prog()

    def din(name, shape, dt=F32):
        return nc.dram_tensor(name, list(shape), dt, kind="ExternalInput").ap()

    def dout(name, shape):
        return nc.dram_tensor(name, list(shape), F32, kind="ExternalOutput").ap()

    x_p = din("x_prompt", [NP, SEQ, D])
    x_s = din("x_sample", [NS, DSEQ, D])
    mem_p = din("mem_prompt", [NP, NMEM, D])
    st_hg = din("state_hgrn", [DEPTH, NS, 4, 128, 128])
    c_conv = din("cache_conv", [DEPTH, NS, 30, 256])
    c_pool = din("cache_pool", [DEPTH, NS, 15, 256])
    c_mk = din("cache_mem_k", [DEPTH, NS, NMEM, D])
    c_mv = din("cache_mem_v", [DEPTH, NS, NMEM, D])
    w_in = din("w_in", [DEPTH, D, 2816])
    w_out = din("w_out", [DEPTH, D, D])
    xq_w = din("xq_w", [DEPTH, D, D])
    xk_w = din("xk_w", [DEPTH, D, D])
    xv_w = din("xv_w", [DEPTH, D, D])
    xo_w = din("xo_w", [DEPTH, D, D])
    w_up = din("w_up", [DEPTH, D, 4 * D])
    w_down = din("w_down", [DEPTH, 4 * D, D])
    NPP = cfg["NPP"]
    OFF = cfg["OFF"]
    pp_d = din("pp", [128, NPP])
    pwbd_d = din("pwbd", [128, DEPTH, 2, 128])
    cst_f = din("cst_f", [128, 256 + TT])
    cst_b = din("cst_b", [128, 128 * 3 + 64], BF)

    y_p = dout("y_prompt", [NP, SEQ, D])
    y_s = dout("y_sample", [NS, DSEQ, D])
    hg_p = dout("hg_p", [DEPTH, NP, 4, 128, 128])
    conv_p = dout("conv_p", [DEPTH, NP, 30, 256])
    pool_p = dout("pool_p", [DEPTH, NP, 15, 256])
    mk_p = dout("mk_p", [DEPTH, NP, NMEM, D])
    mv_p = dout("mv_p", [DEPTH, NP, NMEM, D])
    hg_s = dout("hg_s", [DEPTH, NS, 4, 128, 128])
    conv_s = dout("conv_s", [DEPTH, NS, 30, 256])
    pool_s = dout("pool_s", [DEPTH, NS, 15, 256])
    wscr = nc.dram_tensor("wscr", [DEPTH * NPL, 128, 8 * 512], BF).ap()
    wscr_b = [Buf("wscr%d" % i) for i in range(DEPTH * NPL)]

    class SB:
        top = 16384 + 128
        cnt = 0

    def salloc(shape, dt, at=None):
        nbytes = int(np.prod(shape)) * (4 if dt == F32 else 2)
        nbytes = (nbytes + 63) // 64 * 64
        if at is None:
            off = SB.top
            SB.top += nbytes
            assert SB.top <= 229376 - 64, ("SBUF overflow", SB.top)
        else:
            off = at
        SB.cnt += 1
        t = nc.alloc_sbuf_tensor_at("t%d" % SB.cnt, [128] + list(shape), dt, offset=off)
        return t.ap(), off + nbytes

    def perm(shape, dt, name):
        ap, _ = salloc(shape, dt)
        return ap, Buf(name)

    xT, xT_b = perm([8, TT], F32, "xT")
    xTc = [Buf("xT%d" % c) for c in range(8)]
    hT, _ = perm([8, TT], BF, "hT")
    hTc = [Buf("hT%d" % c) for c in range(8)]
    rstd, rstd_b = perm([TT], F32, "rstd")
    mixT, _ = perm([8, TT], BF, "mixT")
    mix_b = [Buf("mix%d" % c) for c in range(8)]
    sqs = hT
    slots = [perm([8, 512], BF, "slot%d" % i) for i in range(NSLOT)]
    mkT, _ = perm([DEPTH, 8, NMEM], BF, "mkT")
    mvS, _ = perm([DEPTH, 2, D], BF, "mv")
    mk_b = [Buf("mk%d" % l) for l in range(DEPTH)]
    mv_b = [Buf("mv%d" % l) for l in range(DEPTH)]
    Sst, _ = perm([DEPTH, 4, 128], F32, "S")
    S_b = [[Buf("S%d_%d" % (l, h)) for h in range(4)] for l in range(DEPTH)]
    chist, _ = perm([DEPTH, 2, 30], F32, "chist")
    phist, _ = perm([DEPTH, 2, 15], F32, "phist")
    ch_b = [Buf("ch%d" % l) for l in range(DEPTH)]
    ph_b = [Buf("ph%d" % l) for l in range(DEPTH)]
    ppS, pp_b = perm([NPP], F32, "pp")
    pwbd, pwbd_b = perm([DEPTH, 2, 128], BF, "pwbd")
    cf, cf_b = perm([256 + TT], F32, "cf")
    cb, cb_b = perm([128 * 3 + 64], BF, "cb")
    lbv, lb_b = perm([3, DEPTH, 4], F32, "lbv")
    lbt, lbt_b = perm([3, DEPTH, 4], F32, "lbt")
    identf = cf[:, 0:128]
    rmask = cf[:, 128:128 + TT]
    ones256f = cf[:, 128 + TT:256 + TT]
    identb = cb[:, 0:128]
    onesb = cb[:, 128:256]
    ones256 = cb[:, 256:384]
    cmask = cb[0:64, 384:448]
    ARENA0 = SB.top
    ARENA_END = 229376 - 64

    class AR:
        top = ARENA0
        init = ()

    def arena_reset():
        AR.top = ARENA0
        AR.init = tuple(P.fence())

    def aalloc(shape, dt, name):
        ap, end = salloc(shape, dt, at=AR.top)
        AR.top = end
        assert AR.top <= ARENA_END, ("arena overflow", name, AR.top - ARENA0)
        return ap, Buf(name, AR.init)

    ps_t = nc.alloc_psum_tensor("ps", [128, 8, 512], F32)
    ps = ps_t.ap()
    pb = [Buf("bank%d" % i, excl=True) for i in range(8)]

    class PSR:
        lo = 0
        hi = 0
        all = 0

    def bank_lo():
        b = PSR.lo % 4
        PSR.lo += 1
        return b

    def bank8():
        b = PSR.all % 8
        PSR.all += 1
        return b

    def bank_hi():
        b = 4 + PSR.hi % 4
        PSR.hi += 1
        return b

    def MM(out, lhsT, rhs, start, stop, r, w):
        P.add("pe", lambda e: e.matmul(out, lhsT, rhs, start=start, stop=stop), r=r, w=w)

    def TR(out, in_, ident, r, w):
        P.add("pe", lambda e: e.transpose(out, in_, ident), r=r, w=w)

    def ACT(out, in_, func, r, w, bias=0.0, scale=1.0):
        P.add("act", lambda e: e.activation(out, in_, func, bias=bias, scale=scale), r=r, w=w)

    def TT_(eng, out, in0, in1, op, r, w):
        P.add(eng, lambda e: e.tensor_tensor(out, in0, in1, op), r=r, w=w)

    def TS(eng, out, in0, s1, s2, op0, op1, r, w):
        if s2 is None:
            P.add(eng, lambda e: e.tensor_scalar(out, in0, s1, None, op0), r=r, w=w)
        else:
            P.add(eng, lambda e: e.tensor_scalar(out, in0, s1, s2, op0, op1), r=r, w=w)

    def STT(eng, out, in0, sc, in1, op0, op1, r, w):
        P.add(eng, lambda e: e.scalar_tensor_tensor(out, in0, sc, in1, op0, op1), r=r, w=w)

    def CP(eng, out, in_, r, w):
        P.add(eng, lambda e: e.tensor_copy(out, in_), r=r, w=w)

    def MSET(eng, ap, val, w):
        P.add(eng, lambda e: e.memset(ap, val), w=w)

    ioring = [0]

    def DMA(out, in_, r, w, eng="sp", grp=None):
        if grp is None:
            grp = "io%d" % (ioring[0] % 8)
            ioring[0] += 1
        P.add(eng, lambda e: e.dma_start(out=out, in_=in_), r=r, w=w, dma=grp)

    def pcol(name, l=0, j=0):
        o = OFF[name] + j + (0 if name in ("final_g", "invw", "rc16", "eps") else l * cfg["PPL"])
        return ppS[:, o:o + 1]

    DMA(ppS, pp_d, r=[], w=[pp_b])
    DMA(cf, cst_f, r=[], w=[cf_b])
    DMA(cb, cst_b, r=[], w=[cb_b])
    DMA(pwbd, pwbd_d, r=[], w=[pwbd_b], eng="pool", grp="cast0")

    def wsrc(l, name):
        if name.startswith("in"):
            j = int(name[2])
            wd = 512 if j < 5 else 256
            return w_in[l].rearrange("(kc p) n -> p kc n", p=128)[:, :, j * 512:j * 512 + wd], wd
        if name.startswith("out"):
            src = w_out
        elif name[0] == "q":
            src = xq_w
        elif name[0] == "o":
            src = xo_w
        elif name[0] == "k":
            src = xk_w
        elif name[0] == "v":
            src = xv_w
        elif name[0] == "u":
            j = int(name[1])
            return w_up[l].rearrange("(kc p) n -> p kc n", p=128)[:, :, j * 512:(j + 1) * 512], 512
        elif name[0] == "d":
            kg, half = int(name[1]), int(name[2])
            return (w_down[l][kg * 1024:(kg + 1) * 1024, :].rearrange("(kc p) n -> p kc n", p=128)
                    [:, :, half * 512:(half + 1) * 512]), 512
        j = int(name[-1])
        return src[l].rearrange("(kc p) n -> p kc n", p=128)[:, :, j * 512:(j + 1) * 512], 512

    def cast_all(order):
        castn = 0
        seen = set()
        for (l, name) in order:
            if (l, name) in seen:
                continue
            seen.add((l, name))
            i = l * NPL + PIDX[name]
            src, wd = wsrc(l, name)
            dst = wscr[i].rearrange("p (kc n) -> p kc n", n=512)[:, :, 0:wd]
            DMA(dst, src, r=[], w=[wscr_b[i]], eng="pool", grp="cast%d" % (castn % 8))
            castn += 1

    lbp = ppS[:, OFF["lbp"]:OFF["lbp"] + DEPTH * 4].rearrange("p (l h) -> p l h", h=4)
    ex = lbt[:, 0]
    ACT(ex, lbp, AF.Exp, r=[pp_b], w=[lbt_b])
    den = lbt[:, 1, 0]
    CP("dve", den, ex[:, 0], r=[lbt_b], w=[lbt_b])
    for l in range(1, DEPTH):
        TT_("dve", den, den, ex[:, l], ALU.add, r=[lbt_b], w=[lbt_b])
    P.add("dve", lambda e: e.reciprocal(den, den), r=[lbt_b], w=[lbt_b])
    MSET("dve", lbv[:, 0, 0], 0.0, w=[lb_b])
    for l in range(1, DEPTH):
        TT_("dve", lbt[:, 2, 0], ex[:, l], den, ALU.mult, r=[lbt_b], w=[lbt_b])
        TT_("dve", lbv[:, 0, l], lbv[:, 0, l - 1], lbt[:, 2, 0], ALU.add, r=[lbt_b, lb_b], w=[lb_b])
    TS("dve", lbv[:, 1], lbv[:, 0], -1.0, 1.0, ALU.mult, ALU.add, r=[lb_b], w=[lb_b])
    TS("dve", lbv[:, 2], lbv[:, 1], -1.0, None, ALU.mult, None, r=[lb_b], w=[lb_b])

    seqs = [("p", b) for b in range(NP)] + [("s", b) for b in range(NS)]

    def piece_plan():
        plan = []
        for kind, b in seqs:
            T = TT if kind == "p" else DSEQ
            ntile = (SEQ // TT) if kind == "p" else 1
            if kind == "p":
                for l in range(DEPTH):
                    plan += [(l, n) for n in ("k0", "k1", "v0", "v1")]
            for t in range(ntile):
                for l in range(DEPTH):
                    plan += [(l, n) for n in PIECES[:28]]
        return plan

    plan = piece_plan()
    U = []
    for pc in plan:
        if pc not in U:
            U.append(pc)
    upos = {pc: i for i, pc in enumerate(U)}

    class CQ:
        done = 0

    def ensure_cast(upto):
        while CQ.done < min(upto + 1, len(U)):
            l, name = U[CQ.done]
            i = l * NPL + PIDX[name]
            src, wd = wsrc(l, name)
            dst = wscr[i].rearrange("p (kc n) -> p kc n", n=512)[:, :, 0:wd]
            DMA(dst, src, r=[], w=[wscr_b[i]], eng="pool", grp="cast%d" % (CQ.done % 8))
            CQ.done += 1

    class WQ:
        nxt = 0
        issued = 0

    def wget(l, name):
        i = WQ.nxt
        assert plan[i] == (l, name), (plan[i], l, name)
        while WQ.issued < len(plan) and WQ.issued < i + NSLOT:
            j = WQ.issued
            pl, pn = plan[j]
            gi = pl * NPL + PIDX[pn]
            ensure_cast(upos[(pl, pn)] + 6)
            sap, sbuf = slots[j % NSLOT]
            if pn == "in5":
                DMA(sap[:, :, 0:256], wscr[gi].rearrange("p (kc n) -> p kc n", n=512)[:, :, 0:256],
                    r=[wscr_b[gi]], w=[sbuf], grp="w%d" % (j % NSLOT))
            else:
                DMA(sap.rearrange("p kc n -> p (kc n)"), wscr[gi], r=[wscr_b[gi]], w=[sbuf], grp="w%d" % (j % NSLOT))
            WQ.issued += 1
        WQ.nxt += 1
        return slots[i % NSLOT]

    class NS:
        bank = None
        pend = []
        cnt = 0

    def rms_finish(T, bk, inv_n):
        ACT(rstd[:, 0:T], ps[:, bk, 0:T], AF.Ln, r=[pb[bk], pp_b], w=[rstd_b], bias=pcol("eps"), scale=inv_n)
        ACT(rstd[:, 0:T], rstd[:, 0:T], AF.Exp, r=[rstd_b], w=[rstd_b], scale=-0.5)

    def rms_stats_bulk(T):
        ACT(sqs[:, :, 0:T], xT[:, :, 0:T], AF.Square, r=xTc, w=hTc)
        bk = bank_lo()
        for c in range(8):
            MM(ps[:, bk, 0:T], onesb, sqs[:, c, 0:T], c == 0, c == 7, r=[hTc[c], cb_b], w=[pb[bk]])
        return bk

    def norm_begin(bk):
        NS.bank, NS.pend, NS.cnt = bk, [], 0

    def norm_push(T, c):
        ACT(sqs[:, c, 0:T], xT[:, c, 0:T], AF.Square, r=[xTc[c]], w=[hTc[c]])
        NS.pend.append(c)
        while len(NS.pend) > 2:
            norm_mm(T, NS.pend.pop(0))

    def norm_mm(T, c):
        MM(ps[:, NS.bank, 0:T], onesb, sqs[:, c, 0:T], NS.cnt == 0, NS.cnt == 7, r=[hTc[c], cb_b], w=[pb[NS.bank]])
        NS.cnt += 1

    def norm_flush(T):
        while NS.pend:
            norm_mm(T, NS.pend.pop(0))
        assert NS.cnt == 8
        bk = NS.bank
        NS.bank = None
        return bk

    def norm_to_hT(T, l, gname, bk=None):
        if bk is None:
            bk = rms_stats_bulk(T)
        rms_finish(T, bk, 1.0 / D)
        for c in range(8):
            STT("dve", hT[:, c, 0:T], xT[:, c, 0:T], pcol(gname, l, c), rstd[:, 0:T],
                ALU.mult, ALU.mult, r=[xTc[c], rstd_b, pp_b], w=[hTc[c]])

    def proj_fm(T, l, name, nch, rhsT, rhs_b, consume, banks=None):
        sap, sbuf = wget(l, name)
        for m in range(nch):
            if banks is None:
                bk = bank8()
            else:
                bk = banks[PSR.all % len(banks)]
                PSR.all += 1
            for kc in range(8):
                MM(ps[:, bk, 0:T], sap[:, kc, m * 128:(m + 1) * 128], rhsT[:, kc, 0:T], kc == 0, kc == 7,
                   r=[sbuf, rhs_b[kc]], w=[pb[bk]])
            consume(m, bk)

    def add_to_x(T, base):
        def f(m, bk):
            c = base + m
            TT_("dve", xT[:, c, 0:T], xT[:, c, 0:T], ps[:, bk, 0:T], ALU.add, r=[pb[bk], xTc[c]], w=[xTc[c]])
            if NS.bank is not None:
                norm_push(T, c)
        return f

    def layer(T, l, first_tile, nbk_in=None):
        NCH = T // 64
        L30 = 30 + T
        L15 = 15 + T
        arena_reset()
        vS, v_b = aalloc([8, 512], BF, "v")
        qp, qp_b = aalloc([4, TT], BF, "qp")
        kp, kp_b = aalloc([4, TT], BF, "kp")
        sog, sog_b = aalloc([4, TT], BF, "sog")
        S0b = [[aalloc([128], BF, "S0b%d_%d" % (h, i)) for i in range(2)] for h in range(4)]
        E1 = [aalloc([TT], F32, "E1_%d" % h) for h in range(4)]
        csg = [aalloc([8], F32, "cs%d" % h) for h in range(4)]
        dpool, dp_b = aalloc([2, TT], BF, "dpool")
        fc32, fc32_b = aalloc([2, 30 + TT], F32, "fc32")
        fcb, fcb_b = aalloc([2, 30 + TT], BF, "fcb")
        diag, diag_b = aalloc([2, 31, 128], BF, "diag")
        sg, sg_b = aalloc([2, TT], F32, "sg")
        dwb32, dwb32_b = aalloc([2, TT], F32, "dwb32")
        dsq, dsq_b = aalloc([2, TT], BF, "dsq")
        ex2s, ex2s_b = aalloc([TT], F32, "ex2s")
        mean_s, mean_sb = sg[:, 0, :], sg_b
        rs, rs_b = sg[:, 1, :], sg_b
        TMP0 = AR.top
        P.cur = "norm_mix"
        norm_to_hT(T, l, "g_mix", nbk_in)

        P.cur = "pool"
        fullp, fullp_b = aalloc([2, 15 + TT], F32, "fullp")
        s2, s2_b = aalloc([2, 15 + TT], F32, "s2")
        s4, s4_b = aalloc([15 + TT], F32, "s4")
        s8, s8_b = aalloc([15 + TT], F32, "s8")
        wsum, ws_b = aalloc([2, TT], F32, "ws")
        CP("pool", fullp[:, :, 0:15], phist[:, l], r=[ph_b[l]], w=[fullp_b])
        CP("pool", fc32[:, :, 0:30], chist[:, l], r=[ch_b[l]], w=[fc32_b])
        o = OFF["conv_w"] + l * cfg["PPL"]
        TT_("dve", diag.rearrange("p c j k -> p (c j) k"),
            identb.rearrange("p (o k) -> p o k", o=1).broadcast_to([128, 62, 128]),
            ppS[:, o:o + 62].rearrange("p (j o) -> p j o", o=1).broadcast_to([128, 62, 128]), ALU.mult,
            r=[cb_b, pp_b], w=[diag_b])

        def c_pu(m, bk):
            ACT(fullp[:, m, 15:L15], ps[:, bk, 0:T], AF.Copy, r=[pb[bk]], w=[fullp_b])
        proj_fm(T, l, "in5", 2, hT, hTc, c_pu)
        TT_("pool", s2[:, :, 1:L15], fullp[:, :, 1:L15], fullp[:, :, 0:L15 - 1], ALU.add, r=[fullp_b], w=[s2_b])
        TT_("pool", s4[:, 3:L15], s2[:, 1, 3:L15], s2[:, 1, 1:L15 - 2], ALU.add, r=[s2_b], w=[s4_b])
        TT_("pool", s8[64:128, 7:L15], s4[64:128, 7:L15], s4[64:128, 3:L15 - 4], ALU.add, r=[s4_b], w=[s8_b])
        TT_("pool", wsum[0:64, 0, 0:T], fullp[0:64, 0, 15:L15], fullp[0:64, 0, 14:L15 - 1], ALU.add,
            r=[fullp_b], w=[ws_b])
        TT_("pool", wsum[64:128, 0, 0:T], s2[64:128, 0, 15:L15], s2[64:128, 0, 13:L15 - 2], ALU.add,
            r=[s2_b], w=[ws_b])
        TT_("pool", wsum[0:64, 1, 0:T], s4[0:64, 15:L15], s4[0:64, 11:L15 - 4], ALU.add, r=[s4_b], w=[ws_b])
        TT_("pool", wsum[64:128, 1, 0:T], s8[64:128, 15:L15], s8[64:128, 7:L15 - 8], ALU.add, r=[s8_b], w=[ws_b])
        CP("pool", phist[:, l], fullp[:, :, T:T + 15], r=[fullp_b], w=[ph_b[l]])
        if first_tile:
            rc = ppS[:, OFF["rc16"]:OFF["rc16"] + 32].rearrange("p (c t) -> p c t", t=16)
            TT_("pool", wsum[:, :, 0:16], wsum[:, :, 0:16], rc, ALU.mult, r=[ws_b, pp_b], w=[ws_b])

        P.cur = "conv"
        cab = [None, None]

        def c_cacg(m, bk):
            if m < 2:
                cab[m] = bk
            else:
                ch = m - 2
                ACT(sg[:, ch, 0:T], ps[:, bk, 0:T], AF.Sigmoid, r=[pb[bk]], w=[sg_b])
                TT_("dve", fc32[:, ch, 30:L30], ps[:, cab[ch], 0:T], sg[:, ch, 0:T], ALU.mult,
                    r=[pb[cab[ch]], sg_b], w=[fc32_b])
        proj_fm(T, l, "in4", 4, hT, hTc, c_cacg)
        CP("pool", fcb[:, :, 0:L30], fc32[:, :, 0:L30], r=[fc32_b], w=[fcb_b])
        CP("pool", chist[:, l], fc32[:, :, T:T + 30], r=[fc32_b], w=[ch_b[l]])
        P.cur = "pool"
        for ch in range(2):
            STT("dve", dpool[:, ch, 0:T], wsum[:, ch, 0:T], pcol("invw", 0, ch), fullp[:, ch, 15:L15],
                ALU.mult, ALU.subtract, r=[ws_b, fullp_b, pp_b], w=[dp_b])

        P.cur = "hgrn_prep"
        AR.top = TMP0
        AR.init = tuple(P.fence())
        sig = [aalloc([TT], F32, "sig%d" % i) for i in range(4)]
        sq_ = [aalloc([TT], F32, "sq_%d" % i) for i in range(4)]
        lgf = [aalloc([TT], F32, "lgf%d" % i) for i in range(2)]
        Acs = [aalloc([TT], F32, "A%d" % i) for i in range(2)]
        gref = [aalloc([8], F32, "gref%d" % i) for i in range(2)]

        def c_f(h, bk):
            ACT(sig[h][0][:, 0:T], ps[:, bk, 0:T], AF.Sigmoid, r=[pb[bk]], w=[sig[h][1]])
        proj_fm(T, l, "in1", 4, hT, hTc, c_f)

        def chain_f(h):
            i = h % 2
            (sg_, sgb), (lg_, lgb), (A_, Ab) = sig[h], lgf[i], Acs[i]
            e1, e1b = E1[h]
            gr, grb = gref[i]
            cs, csb = csg[h]
            ACT(lg_[:, 0:T], sg_[:, 0:T], AF.Ln, r=[sgb, lb_b], w=[lgb], bias=lbv[:, 0, l, h:h + 1],
                scale=lbv[:, 1, l, h:h + 1])
            P.add("dve", lambda e: e.tensor_tensor_scan(A_[:, 0:T], rmask[:, 0:T], lg_[:, 0:T], 0.0, ALU.mult, ALU.add),
                  r=[lgb, cf_b], w=[Ab])
            A3 = A_[:, 0:T].rearrange("p (c t) -> p c t", t=64)
            TT_("dve", lg_[:, 0:T].rearrange("p (c t) -> p c t", t=64), A3,
                A3[:, :, 31:32].broadcast_to([128, NCH, 64]), ALU.subtract, r=[Ab], w=[lgb])
            ACT(gr[:, 0:NCH], A3[:, :, 31], AF.Exp, r=[Ab], w=[grb])
            ACT(e1[:, 0:T], lg_[:, 0:T], AF.Exp, r=[lgb], w=[e1b])
            ACT(A_[:, 0:T], lg_[:, 0:T], AF.Exp, r=[lgb], w=[Ab], scale=-1.0)
            TS("dve", sg_[:, 0:T], sg_[:, 0:T], lbv[:, 2, l, h:h + 1], lbv[:, 1, l, h:h + 1], ALU.mult, ALU.add,
               r=[sgb, lb_b], w=[sgb])
            TT_("pool", kp[:, h, 0:T], sg_[:, 0:T], A_[:, 0:T], ALU.mult, r=[sgb, Ab], w=[kp_b])
            CP("dve", cs[:, 0:1], gr[:, 0:1], r=[grb], w=[csb])
            if NCH > 1:
                e13 = e1[:, 0:T].rearrange("p (c t) -> p c t", t=64)
                TT_("dve", cs[:, 1:NCH], gr[:, 1:NCH], e13[:, 0:NCH - 1, 63], ALU.mult, r=[grb, e1b], w=[csb])

        def c_og(h, bk):
            ACT(sog[:, h, 0:T], ps[:, bk, 0:T], AF.Silu, r=[pb[bk]], w=[sog_b])
        proj_fm(T, l, "in3", 4, hT, hTc, c_og)

        def c_q(h, bk):
            ACT(sq_[h][0][:, 0:T], ps[:, bk, 0:T], AF.Silu, r=[pb[bk]], w=[sq_[h][1]])
        proj_fm(T, l, "in0", 4, hT, hTc, c_q)
        for h in range(4):
            chain_f(h)
        P.cur = "pool"
        for ch in range(2):
            bk = bank8()
            MM(ps[:, bk, 0:T], pwbd[:, l, ch, :], dpool[:, ch, 0:T], True, True, r=[dp_b, pwbd_b], w=[pb[bk]])
            ACT(mixT[:, 6 + ch, 0:T], ps[:, bk, 0:T], AF.Copy, r=[pb[bk], pp_b], w=[mix_b[6 + ch]],
                scale=pcol("pool_scale", l, ch))
        P.cur = "hgrn_prep"

        for h in range(4):
            TT_("dve", qp[:, h, 0:T], sq_[h][0][:, 0:T], E1[h][0][:, 0:T], ALU.mult,
                r=[sq_[h][1], E1[h][1]], w=[qp_b])
        conv_tasks = []

        def conv_mm(ch, j):
            def f():
                MM(ps[:, ch, 0:T], diag[:, ch, j, :], fcb[:, ch, j:j + T], j == 0, j == 30,
                   r=[diag_b, fcb_b], w=[pb[ch]])
            return f

        def conv_tail():
            for ch in range(2):
                ACT(dwb32[:, ch, 0:T], ps[:, ch, 0:T], AF.Identity, r=[pb[ch], pp_b], w=[dwb32_b],
                    bias=pcol("conv_b", l, ch))
                ACT(dsq[:, ch, 0:T], ps[:, ch, 0:T], AF.Square, r=[pb[ch], pp_b], w=[dsq_b],
                    bias=pcol("conv_b", l, ch))

        def conv_stats():
            bm, be = 0, 1
            for ch in range(2):
                MM(ps[:, bm, 0:T], ones256f, dwb32[:, ch, 0:T], ch == 0, ch == 1, r=[dwb32_b, cf_b], w=[pb[bm]])
            for ch in range(2):
                MM(ps[:, be, 0:T], ones256, dsq[:, ch, 0:T], ch == 0, ch == 1, r=[dsq_b, cb_b], w=[pb[be]])
            ACT(mean_s[:, 0:T], ps[:, bm, 0:T], AF.Copy, r=[pb[bm]], w=[mean_sb])
            ACT(ex2s[:, 0:T], ps[:, be, 0:T], AF.Copy, r=[pb[be]], w=[ex2s_b])
            TT_("pool", rs[:, 0:T], mean_s[:, 0:T], mean_s[:, 0:T], ALU.mult, r=[mean_sb], w=[rs_b])
            TT_("pool", rs[:, 0:T], ex2s[:, 0:T], rs[:, 0:T], ALU.subtract, r=[ex2s_b, rs_b], w=[rs_b])
            ACT(rs[:, 0:T], rs[:, 0:T], AF.Ln, r=[rs_b, pp_b], w=[rs_b], bias=pcol("eps"))
            ACT(rs[:, 0:T], rs[:, 0:T], AF.Exp, r=[rs_b], w=[rs_b], scale=-0.5)
            for ch in range(2):
                TT_("pool", dwb32[:, ch, 0:T], dwb32[:, ch, 0:T], mean_s[:, 0:T], ALU.subtract,
                    r=[dwb32_b, mean_sb], w=[dwb32_b])
                TT_("pool", dwb32[:, ch, 0:T], dwb32[:, ch, 0:T], rs[:, 0:T], ALU.mult, r=[dwb32_b, rs_b], w=[dwb32_b])
                ACT(mixT[:, 4 + ch, 0:T], dwb32[:, ch, 0:T], AF.Silu, r=[dwb32_b, pp_b], w=[mix_b[4 + ch]],
                    bias=pcol("conv_ln_b", l, ch), scale=pcol("conv_ln_g", l, ch))
        for ch in range(2):
            for j in range(31):
                conv_tasks.append(conv_mm(ch, j))
        conv_tasks.append(conv_tail)
        for _ in range(12):
            conv_tasks.append(None)
        conv_tasks.append(conv_stats)

        def conv_step(n):
            for _ in range(n):
                if conv_tasks:
                    t_ = conv_tasks.pop(0)
                    if t_ is not None:
                        P.cur = "conv"
                        t_()
                        P.cur = "hgrn_chunks"

        conv_step(62)
        sap, sbuf = wget(l, "in2")
        for c in range(NCH):
            bk = 2 + c % 6
            for kc in range(8):
                MM(ps[0:64, bk, 0:512], hT[:, kc, c * 64:(c + 1) * 64], sap[:, kc, :], kc == 0, kc == 7,
                   r=[sbuf, hTc[kc]], w=[pb[bk]])
            if c % 2 == 0:
                ACT(vS[0:64, c, :], ps[0:64, bk, 0:512], AF.Copy, r=[pb[bk]], w=[v_b])
            else:
                CP("dve", vS[0:64, c, :], ps[0:64, bk, 0:512], r=[pb[bk]], w=[v_b])

        P.cur = "hgrn_chunks"
        AR.top = TMP0
        AR.init = tuple(P.fence())
        scm, scm_b = aalloc([4, 8, 64], BF, "scm")
        ktm, ktm_b = aalloc([4, 8, 128], BF, "ktm")
        osq, osq_b = aalloc([2, TT], BF, "osq")
        ro, ro_b = aalloc([2, TT], F32, "ro")
        otmp, otmp_b = aalloc([2, TT], F32, "otmp")
        for h in range(4):
            bA = 2
            for c in range(NCH):
                cs_ = slice(c * 64, (c + 1) * 64)
                MM(ps[0:64, bA, cs_], kp[:, h, cs_], qp[:, h, cs_], True, True, r=[kp_b, qp_b], w=[pb[bA]])
            TT_("dve", scm[0:64, h, 0:NCH, :], ps[0:64, bA, 0:T].rearrange("p (c t) -> p c t", t=64),
                cmask.rearrange("p (o t) -> p o t", o=1).broadcast_to([64, NCH, 64]), ALU.mult,
                r=[pb[bA], cb_b], w=[scm_b])
            bB = 3
            psb = ps[:, bB, :].bitcast(BF)
            conv_step(6)
            for c in range(NCH):
                TR(psb[0:64, c * 128:(c + 1) * 128], kp[:, h, c * 64:(c + 1) * 64], identb,
                   r=[kp_b, cb_b], w=[pb[bB]])
            ACT(ktm[0:64, h, 0:NCH, :], psb[0:64, 0:NCH * 128].rearrange("p (c d) -> p c d", d=128), AF.Copy,
                r=[pb[bB]], w=[ktm_b])
        ob = [4, 5, 6, 7]

        def emit_dS(c):
            bk = 2 + c % 2
            for h in range(4):
                MM(ps[:, bk, h * 128:(h + 1) * 128], ktm[0:64, h, c, :], vS[0:64, c, h * 128:(h + 1) * 128],
                   True, True, r=[ktm_b, v_b], w=[pb[bk]])
        emit_dS(0)
        for h in range(4):
            s0, s0b = S0b[h][0]
            TS("dve", s0, Sst[:, l, h, :], csg[h][0][:, 0:1], None, ALU.mult, None,
               r=[S_b[l][h], csg[h][1]], w=[s0b])
        for c in range(NCH):
            cs_ = slice(c * 64, (c + 1) * 64)
            if c + 1 < NCH:
                emit_dS(c + 1)
            for h in range(4):
                s0, s0b = S0b[h][c % 2]
                cs, csb = csg[h]
                Sh = Sst[:, l, h, :]
                vh = vS[0:64, c, h * 128:(h + 1) * 128]
                MM(ps[:, ob[h], cs_], vh, scm[0:64, h, c, :], True, False, r=[v_b, scm_b], w=[pb[ob[h]]])
                MM(ps[:, ob[h], cs_], s0, qp[:, h, cs_], False, True, r=[s0b, qp_b], w=[pb[ob[h]]])
                b3 = 2 + c % 2
                STT("dve", Sh, Sh, cs[:, c:c + 1], ps[:, b3, h * 128:(h + 1) * 128], ALU.mult, ALU.add,
                    r=[S_b[l][h], csb, pb[b3]], w=[S_b[l][h]])
                if c + 1 < NCH:
                    s1, s1b = S0b[h][(c + 1) % 2]
                    TS("dve", s1, Sh, cs[:, c + 1:c + 2], None, ALU.mult, None, r=[S_b[l][h], csb], w=[s1b])
                conv_step(4)
        conv_step(1000)
        P.cur = "hgrn_out"
        for h in range(4):
            Sh = Sst[:, l, h, :]
            TS("dve", Sh, Sh, E1[h][0][:, T - 1:T], None, ALU.mult, None, r=[S_b[l][h], E1[h][1]], w=[S_b[l][h]])
        for g in range(2):
            o0 = 4 + 2 * g
            ACT(osq[:, :, 0:T], ps[:, o0:o0 + 2, 0:T], AF.Square, r=[pb[o0], pb[o0 + 1]], w=[osq_b])
            n0 = 2
            for j in range(2):
                MM(ps[:, n0 + j, 0:T], onesb, osq[:, j, 0:T], True, True, r=[osq_b, cb_b], w=[pb[n0 + j]])
            for j in range(2):
                h = 2 * g + j
                STT("dve", otmp[:, j, 0:T], ps[:, o0 + j, 0:T], pcol("hg_norm_g", l, h), sog[:, h, 0:T],
                    ALU.mult, ALU.mult, r=[pb[o0 + j], sog_b, pp_b], w=[otmp_b])
            ACT(ro[:, :, 0:T], ps[:, n0:n0 + 2, 0:T], AF.Ln, r=[pb[n0], pb[n0 + 1], pp_b], w=[ro_b],
                bias=pcol("eps"), scale=1.0 / 128)
            ACT(ro[:, :, 0:T], ro[:, :, 0:T], AF.Exp, r=[ro_b], w=[ro_b], scale=-0.5)
            TT_("dve", mixT[:, 2 * g:2 * g + 2, 0:T], otmp[:, :, 0:T], ro[:, :, 0:T], ALU.mult,
                r=[otmp_b, ro_b], w=[mix_b[2 * g], mix_b[2 * g + 1]])
        P.cur = "w_out"
        norm_begin(7)
        proj_fm(T, l, "out0", 4, mixT, mix_b, add_to_x(T, 0), banks=[0, 1, 2, 3, 4, 5, 6])
        proj_fm(T, l, "out1", 4, mixT, mix_b, add_to_x(T, 4), banks=[0, 1, 2, 3, 4, 5, 6])
        nbk = norm_flush(T)

        arena_reset()
        qT, qT_b = aalloc([8, TT], BF, "qT")
        ee = [aalloc([2, TT], BF, "e%d" % i) for i in range(2)]
        rden = [aalloc([TT], F32, "rden%d" % i) for i in range(2)]
        P.cur = "norm_x"
        norm_to_hT(T, l, "g_x", nbk)

        def c_qx(base):
            def f(m, bk):
                ACT(qT[:, base + m, 0:T], ps[:, bk, 0:T], AF.Copy, r=[pb[bk]], w=[qT_b])
            return f
        P.cur = "xq"
        proj_fm(T, l, "q0", 4, hT, hTc, c_qx(0))
        proj_fm(T, l, "q1", 4, hT, hTc, c_qx(4))
        P.cur = "attn"
        for hh in range(4):
            e_, eb = ee[hh % 2]
            rd, rdb = rden[hh % 2]
            for mc in range(2):
                bk = bank_lo()
                for dc in range(2):
                    MM(ps[:, bk, 0:T], mkT[:, l, 2 * hh + dc, mc * 128:(mc + 1) * 128], qT[:, 2 * hh + dc, 0:T],
                       dc == 0, dc == 1, r=[mk_b[l], qT_b], w=[pb[bk]])
                ACT(e_[:, mc, 0:T], ps[:, bk, 0:T], AF.Exp, r=[pb[bk]], w=[eb], scale=1.0 / 16.0)
            bd = bank_hi()
            for mc in range(2):
                MM(ps[:, bd, 0:T], onesb, e_[:, mc, 0:T], mc == 0, mc == 1, r=[eb, cb_b], w=[pb[bd]])
            P.add("dve", lambda e, rd=rd, bd=bd: e.reciprocal(rd[:, 0:T], ps[:, bd, 0:T]), r=[pb[bd]], w=[rdb])
            for dc in range(2):
                bk = bank_lo()
                for mc in range(2):
                    MM(ps[:, bk, 0:T], mvS[:, l, mc, hh * 256 + dc * 128:hh * 256 + (dc + 1) * 128], e_[:, mc, 0:T],
                       mc == 0, mc == 1, r=[mv_b[l], eb], w=[pb[bk]])
                TT_("dve", mixT[:, 2 * hh + dc, 0:T], ps[:, bk, 0:T], rd[:, 0:T], ALU.mult,
                    r=[pb[bk], rdb], w=[mix_b[2 * hh + dc]])
        P.cur = "xo"
        norm_begin(7)
        proj_fm(T, l, "o0", 4, mixT, mix_b, add_to_x(T, 0), banks=[0, 1, 2, 3, 4, 5, 6])
        proj_fm(T, l, "o1", 4, mixT, mix_b, add_to_x(T, 4), banks=[0, 1, 2, 3, 4, 5, 6])
        nbk = norm_flush(T)

        arena_reset()
        hid = [aalloc([16, TT], BF, "hid%d" % i) for i in range(2)]
        rl = [aalloc([TT], BF, "rl%d" % i) for i in range(3)]
        P.cur = "norm_ffn"
        norm_to_hT(T, l, "g_ffn", nbk)
        P.cur = "ffn"
        cnt = [0]
        for half in range(2):
            hd, hdb = hid[half]

            def c_up(j0):
                def f(m, bk):
                    r_, rb_ = rl[cnt[0] % 3]
                    cnt[0] += 1
                    ACT(r_[:, 0:T], ps[:, bk, 0:T], AF.Relu, r=[pb[bk]], w=[rb_])
                    TT_("dve", hd[:, j0 + m, 0:T], r_[:, 0:T], r_[:, 0:T], ALU.mult, r=[rb_], w=[hdb])
                return f
            for j in range(4):
                proj_fm(T, l, "u%d" % (half * 4 + j), 4, hT, hTc, c_up(j * 4))
            for colh in range(2):
                if half == 1 and colh == 0:
                    norm_begin(3)
                bks = [bank_hi() for m in range(4)]
                for kg in range(2):
                    sap, sbuf = wget(l, "d%d%d" % (half * 2 + kg, colh))
                    for m in range(4):
                        for kc in range(8):
                            MM(ps[:, bks[m], 0:T], sap[:, kc, m * 128:(m + 1) * 128], hd[:, kg * 8 + kc, 0:T],
                               kg == 0 and kc == 0, kg == 1 and kc == 7, r=[sbuf, hdb], w=[pb[bks[m]]])
                for m in range(4):
                    add_to_x(T, colh * 4)(m, bks[m])
        return norm_flush(T)

    def load_tile(T, src):
        P.cur = "load"
        arena_reset()
        nb = (T + 127) // 128
        tb = min(T, 128)
        xst, xst_b = aalloc([4, D], F32, "xst")
        DMA(xst[0:tb, 0:nb, :], src.rearrange("(b p) f -> p b f", p=tb), r=[], w=[xst_b])
        for c in range(8):
            bk = bank_lo()
            for b in range(nb):
                TR(ps[:, bk, b * 128:b * 128 + tb], xst[0:tb, b, c * 128:(c + 1) * 128], identf[0:tb, 0:tb],
                   r=[xst_b, cf_b], w=[pb[bk]])
            if c % 2 == 0:
                ACT(xT[:, c, 0:T], ps[:, bk, 0:T], AF.Copy, r=[pb[bk]], w=[xTc[c]])
            else:
                CP("dve", xT[:, c, 0:T], ps[:, bk, 0:T], r=[pb[bk]], w=[xTc[c]])

    def store_tile(T, dst, nbk_in=None):
        P.cur = "store"
        arena_reset()
        nb = (T + 127) // 128
        tb = min(T, 128)
        yT, yT_b = aalloc([8, TT], F32, "yT")
        yst, yst_b = aalloc([4, D], F32, "yst")
        rms_finish(T, nbk_in if nbk_in is not None else rms_stats_bulk(T), 1.0 / D)
        for c in range(8):
            STT("dve", yT[:, c, 0:T], xT[:, c, 0:T], pcol("final_g", 0, c), rstd[:, 0:T],
                ALU.mult, ALU.mult, r=[xTc[c], rstd_b, pp_b], w=[yT_b])
        for b in range(nb):
            for hf in range(2):
                bk = bank_lo()
                for cc in range(4):
                    c = hf * 4 + cc
                    TR(ps[0:tb, bk, cc * 128:(cc + 1) * 128], yT[:, c, b * 128:b * 128 + tb], identf,
                       r=[yT_b, cf_b], w=[pb[bk]])
                if hf == 0:
                    ACT(yst[0:tb, b, 0:512], ps[0:tb, bk, :], AF.Copy, r=[pb[bk]], w=[yst_b])
                else:
                    CP("dve", yst[0:tb, b, 512:1024], ps[0:tb, bk, :], r=[pb[bk]], w=[yst_b])
        DMA(dst.rearrange("(b p) f -> p b f", p=tb), yst[0:tb, 0:nb, :], r=[yst_b], w=[])

    def prep_prompt(b):
        P.cur = "prep_p"
        for l in range(DEPTH):
            for h in range(4):
                MSET("pool", Sst[:, l, h, :], 0.0, w=[S_b[l][h]])
            MSET("pool", chist[:, l], 0.0, w=[ch_b[l]])
            MSET("pool", phist[:, l], 0.0, w=[ph_b[l]])
        arena_reset()
        mst, mst_b = aalloc([2, D], F32, "mst")
        junk, junk_b = aalloc([D], F32, "junk")
        ssm, ssm_b = aalloc([2], F32, "ssm")
        memn, memn_b = aalloc([2, D], BF, "memn")
        mnT, mnT_b = aalloc([8, NMEM], BF, "mnT")
        mhTs = [aalloc([8, NMEM], BF, "mhT%d" % i) for i in range(2)]
        kvst = [aalloc([2, D], F32, "kvst%d" % i) for i in range(2)]
        DMA(mst, mem_p[b].rearrange("(mc p) f -> p mc f", p=128), r=[], w=[mst_b])
        if cfg.get("CUT") == 1:
            return
        for mc in range(2):
            ACT(junk, mst[:, mc, :], AF.Square, r=[mst_b], w=[junk_b])
            P.add("dve", lambda e, mc=mc: e.reduce_sum(ssm[:, mc:mc + 1], junk, AX.X), r=[junk_b], w=[ssm_b])
        ACT(ssm, ssm, AF.Ln, r=[ssm_b, pp_b], w=[ssm_b], bias=pcol("eps"), scale=1.0 / D)
        ACT(ssm, ssm, AF.Exp, r=[ssm_b], w=[ssm_b], scale=-0.5)
        for mc in range(2):
            TS("dve", memn[:, mc, :], mst[:, mc, :], ssm[:, mc:mc + 1], None, ALU.mult, None,
               r=[mst_b, ssm_b], w=[memn_b])
        if cfg.get("CUT") == 2:
            return
        for mc in range(2):
            bk = bank_lo()
            psb = ps[:, bk, :].bitcast(BF)
            for c in range(8):
                TR(psb[:, c * 128:(c + 1) * 128], memn[:, mc, c * 128:(c + 1) * 128], identb,
                   r=[memn_b, cb_b], w=[pb[bk]])
            CP("dve", mnT[:, :, mc * 128:(mc + 1) * 128], psb.rearrange("p (c m) -> p c m", m=128),
               r=[pb[bk]], w=[mnT_b])
        if cfg.get("CUT") == 3:
            return
        for l in range(DEPTH):
            mhT, mhT_b = mhTs[l % 2]
            for c in range(8):
                TS("dve", mhT[:, c, :], mnT[:, c, :], pcol("g_mem", l, c), None,
                   ALU.mult, None, r=[mnT_b, pp_b], w=[mhT_b])
            for wi, (nm, dst, dbuf) in enumerate((("k", mk_p, None), ("v", mv_p, mv_b[l]))):
                st, stb = kvst[wi]
                for half in range(2):
                    sap, sbuf = wget(l, "%s%d" % (nm, half))
                    if cfg.get("CUT") == 4:
                        return
                    if nm == "k":
                        for m in range(4):
                            oc = half * 4 + m
                            bk = bank_lo()
                            for kc in range(8):
                                MM(ps[:, bk, 0:NMEM], sap[:, kc, m * 128:(m + 1) * 128], mhT[:, kc, :], kc == 0, kc == 7,
                                   r=[sbuf, mhT_b], w=[pb[bk]])
                            ACT(mkT[:, l, oc, :], ps[:, bk, 0:NMEM], AF.Copy, r=[pb[bk]], w=[mk_b[l]])
                        if cfg.get("CUT") == 5:
                            return
                    for mc in range(2):
                        bk = bank_lo()
                        for kc in range(8):
                            MM(ps[:, bk, :], mhT[:, kc, mc * 128:(mc + 1) * 128], sap[:, kc, :], kc == 0, kc == 7,
                               r=[sbuf, mhT_b], w=[pb[bk]])
                        CP("dve", st[:, mc, half * 512:(half + 1) * 512], ps[:, bk, :], r=[pb[bk]], w=[stb])
                        if cfg.get("CUT") == 6:
                            return
                        if nm == "v":
                            CP("pool", mvS[:, l, mc, half * 512:(half + 1) * 512],
                               st[:, mc, half * 512:(half + 1) * 512], r=[stb], w=[mv_b[l]])
                if cfg.get("CUT") == 8:
                    return
                DMA(dst[l, b].rearrange("(mc p) f -> p mc f", p=128), st, r=[stb], w=[])
                if cfg.get("CUT") == 7:
                    return
            if cfg.get("CUT") == 9:
                return

    def prep_sample(b):
        P.cur = "prep_s"
        arena_reset()
        kst, kst_b = aalloc([2, D], BF, "kst")
        cst, cst_b = aalloc([256], F32, "cst")
        pst, pst_b = aalloc([256], F32, "pst")
        for l in range(DEPTH):
            DMA(Sst[:, l], st_hg[l, b].rearrange("h d v -> d h v"), r=[], w=S_b[l])
            DMA(mvS[:, l], c_mv[l, b].rearrange("(mc p) f -> p mc f", p=128), r=[], w=[mv_b[l]], eng="pool",
                grp="cast%d" % (l % 8))
            DMA(kst, c_mk[l, b].rearrange("(mc p) f -> p mc f", p=128), r=[], w=[kst_b], eng="pool",
                grp="cast%d" % ((l + 4) % 8))
            for mc in range(2):
                bk = bank_lo()
                psb = ps[:, bk, :].bitcast(BF)
                for oc in range(8):
                    TR(psb[:, oc * 128:(oc + 1) * 128], kst[:, mc, oc * 128:(oc + 1) * 128], identb,
                       r=[kst_b, cb_b], w=[pb[bk]])
                CP("dve", mkT[:, l, :, mc * 128:(mc + 1) * 128], psb.rearrange("p (c m) -> p c m", m=128),
                   r=[pb[bk]], w=[mk_b[l]])
            for (src, n, stg, stgb, hist, hb) in ((c_conv, 30, cst, cst_b, chist, ch_b[l]),
                                                  (c_pool, 15, pst, pst_b, phist, ph_b[l])):
                DMA(stg[0:n, :], src[l, b], r=[], w=[stgb])
                bk = bank_lo()
                for ch in range(2):
                    TR(ps[:, bk, ch * 32:ch * 32 + n], stg[0:n, ch * 128:(ch + 1) * 128], identf[0:n, 0:n],
                       r=[stgb, cf_b], w=[pb[bk]])
                CP("dve", hist[:, l], ps[:, bk, 0:64].rearrange("p (c t) -> p c t", t=32)[:, :, 0:n],
                   r=[pb[bk]], w=[hb])

    def store_states(b, hg_o, conv_o, pool_o):
        P.cur = "store_st"
        arena_reset()
        cso = [aalloc([256], F32, "cso%d" % i) for i in range(2)]
        for l in range(DEPTH):
            DMA(hg_o[l, b].rearrange("h d v -> d h v"), Sst[:, l], r=S_b[l], w=[])
            for i, (dst, n, hist, hb) in enumerate(((conv_o, 30, chist, ch_b[l]), (pool_o, 15, phist, ph_b[l]))):
                so, sob = cso[i]
                bk = bank_lo()
                for ch in range(2):
                    TR(ps[0:n, bk, ch * 128:(ch + 1) * 128], hist[:, l, ch, :], identf, r=[hb, cf_b], w=[pb[bk]])
                CP("dve", so[0:n, :], ps[0:n, bk, 0:256], r=[pb[bk]], w=[sob])
                DMA(dst[l, b], so[0:n, :], r=[sob], w=[])

    STOP = cfg.get("STOP", 99)
    for kind, b in seqs:
        if STOP <= 1:
            break
        if kind == "p":
            prep_prompt(b)
            ntile, T = SEQ // TT, TT
        else:
            prep_sample(b)
            ntile, T = 1, DSEQ
        if STOP <= 2:
            break
        for t in range(ntile):
            src = x_p[b, t * TT:(t + 1) * TT, :] if kind == "p" else x_s[b]
            dst = y_p[b, t * TT:(t + 1) * TT, :] if kind == "p" else y_s[b]
            load_tile(T, src)
            nb_ = None
            if STOP > 3:
                for l in range(DEPTH):
                    nb_ = layer(T, l, kind == "p" and t == 0, nb_)
            store_tile(T, dst, nb_)
        if STOP <= 3:
            continue
        if kind == "p":
            store_states(b, hg_p, conv_p, pool_p)
        else:
            store_states(b, hg_s, conv_s, pool_s)
    assert STOP < 99 or WQ.nxt == len(plan), (WQ.nxt, len(plan))

    engs = ["pe", "act", "dve", "pool", "sp"]
    streams = {e: [] for e in engs}
    cnts = {e: 0 for e in engs}
    for o in P.ops:
        streams[o.eng].append(o)
        if o.dma is None and o.sig:
            cnts[o.eng] += 1
            o.signo = cnts[o.eng]
    import os as _os
    if _os.environ.get("DUMP_LABELS"):
        import json as _json
        _json.dump({e: [o.label for o in streams[e]] for e in engs}, open(_os.environ["DUMP_LABELS"], "w"))
    from contextlib import ExitStack
    with ExitStack() as es:
        esem = {e: es.enter_context(nc.semaphore("s_" + e)) for e in ["pe", "act", "dve", "pool"]}
        dsem = {g: es.enter_context(nc.semaphore("d_" + g)) for g in P.dma_count}
        block = es.enter_context(nc.Block())

        def emit(ename, e):
            have = {}
            for o in streams[ename]:
                for d in sorted(o.deps, key=lambda d: d.id):
                    if d.dma is not None:
                        key, val, sem = "dma:" + d.dma, 16 * d.dmaidx, dsem[d.dma]
                    else:
                        key, val, sem = d.eng, d.signo, esem[d.eng]
                    if have.get(key, 0) >= val:
                        continue
                    have[key] = val
                    e.wait_ge(sem, val)
                ins = o.fn(e)
                if o.dma is not None:
                    ins.then_inc(dsem[o.dma], 16)
                elif o.sig:
                    ins.then_inc(esem[ename], 1)
            if ename == "sp":
                for g, n in P.dma_count.items():
                    e.wait_ge(dsem[g], 16 * n)

        block.tensor(lambda e: emit("pe", e))
        block.scalar(lambda e: emit("act", e))
        block.vector(lambda e: emit("dve", e))
        block.gpsimd(lambda e: emit("pool", e))
        block.sync(lambda e: emit("sp", e))
    return nc, {e: len(streams[e]) for e in engs}


def host_params(cfg, inp):
    DEPTH = cfg["DEPTH"]
    names = [("g_mix", 8), ("g_x", 8), ("g_ffn", 8), ("g_mem", 8), ("hg_norm_g", 4), ("conv_b", 2),
             ("conv_ln_g", 2), ("conv_ln_b", 2), ("pool_scale", 2), ("conv_w", 62)]
    OFF = {}
    o = 0
    for n, k in names:
        OFF[n] = o
        o += k
    PPL = o
    base = PPL * DEPTH
    OFF["final_g"] = base
    OFF["invw"] = base + 8
    OFF["rc16"] = base + 10
    OFF["eps"] = base + 42
    OFF["lbp"] = base + 43
    NPP = base + 43 + DEPTH * 4
    cfg["OFF"], cfg["PPL"], cfg["NPP"] = OFF, PPL, NPP
    if inp is None:
        return None
    pp = np.zeros((128, NPP), np.float32)

    def fm(v, k):
        return np.ascontiguousarray(np.asarray(v, np.float32).reshape(k, 128).T)
    src = dict(g_mix="norm_mix_g", g_x="norm_x_g", g_ffn="norm_ffn_g", g_mem="norm_mem_g", hg_norm_g="hg_norm_g",
               conv_b="conv_b", conv_ln_g="conv_ln_g", conv_ln_b="conv_ln_b", pool_scale="pool_scale")
    for l in range(DEPTH):
        for n, k in names[:-1]:
            pp[:, l * PPL + OFF[n]:l * PPL + OFF[n] + k] = fm(inp[src[n]][l], k)
        cw = np.asarray(inp["conv_w"][l], np.float32)
        for ch in range(2):
            pp[:, l * PPL + OFF["conv_w"] + ch * 31:l * PPL + OFF["conv_w"] + (ch + 1) * 31] = cw[:, ch * 128:(ch + 1) * 128].T
        pp[:, OFF["lbp"] + l * 4:OFF["lbp"] + (l + 1) * 4] = fm(inp["lb_param"][l], 4)
    pp[:, OFF["final_g"]:OFF["final_g"] + 8] = fm(inp["final_g"], 8)
    pp[:, OFF["eps"]] = EPS
    wins = np.array([[2, 8], [4, 16]], np.float32)
    for ch in range(2):
        for hi in range(2):
            w = wins[hi, ch]
            sl = slice(hi * 64, (hi + 1) * 64)
            pp[sl, OFF["invw"] + ch] = 1.0 / w
            pos = np.arange(16, dtype=np.float32)
            pp[sl, OFF["rc16"] + ch * 16:OFF["rc16"] + (ch + 1) * 16] = w / np.minimum(pos + 1.0, w)
    pw = np.asarray(inp["pool_w"], np.float32)
    pwbd = np.zeros((128, DEPTH, 2, 128), np.float32)
    for l in range(DEPTH):
        for g in range(4):
            ch, hi = g // 2, g % 2
            pwbd[hi * 64:(hi + 1) * 64, l, ch, hi * 64:(hi + 1) * 64] = pw[l, g]
    cst_f = np.zeros((128, 256 + TT), np.float32)
    cst_f[:, 0:128] = np.eye(128, dtype=np.float32)
    rm = np.ones(TT, np.float32)
    rm[::64] = 0.0
    cst_f[:, 128:128 + TT] = rm[None, :]
    cst_f[:, 128 + TT:] = 1.0 / 256.0
    cst_b = np.zeros((128, 448), np.float32)
    cst_b[:, 0:128] = np.eye(128)
    cst_b[:, 128:256] = 1.0
    cst_b[:, 256:384] = 1.0 / 256.0
    s = np.arange(64)
    cst_b[0:64, 384:448] = (s[:, None] <= s[None, :]).astype(np.float32)
    return dict(pp=pp, pwbd=pwbd, cst_f=cst_f, cst_b=cst_b.astype(ml_dtypes.bfloat16))


_CACHE = {}


def run(cfg, inp):
    NP, NS, NCORES, DEPTH = cfg["NP"], cfg["NS"], cfg["NCORES"], cfg["DEPTH"]
    hp = host_params(cfg, inp)
    key = tuple(sorted((k, v) for k, v in cfg.items() if k not in ("OFF",)))
    if key not in _CACHE:
        _CACHE[key] = build(cfg)
    nc, stats = _CACHE[key]
    f = lambda a: np.ascontiguousarray(np.asarray(a, np.float32))
    shared = dict(w_in=f(inp["w_in"]), w_out=f(inp["w_out"]), xq_w=f(inp["xq_w"]), xk_w=f(inp["xk_w"]),
                  xv_w=f(inp["xv_w"]), xo_w=f(inp["xo_w"]), w_up=f(inp["w_up"]), w_down=f(inp["w_down"]),
                  pp=hp["pp"], pwbd=hp["pwbd"], cst_f=hp["cst_f"], cst_b=hp["cst_b"])
    in_maps = []
    for c in range(NCORES):
        m = dict(shared)
        m["x_prompt"] = f(inp["x_prompt"][c * NP:(c + 1) * NP])
        m["mem_prompt"] = f(inp["mem_prompt"][c * NP:(c + 1) * NP])
        m["x_sample"] = f(inp["x_sample"][c * NS:(c + 1) * NS])
        m["state_hgrn"] = f(inp["state_hgrn"][:, c * NS:(c + 1) * NS])
        m["cache_conv"] = f(inp["cache_conv"][:, c * NS:(c + 1) * NS])
        m["cache_pool"] = f(inp["cache_pool"][:, c * NS:(c + 1) * NS])
        m["cache_mem_k"] = f(inp["cache_mem_k"][:, c * NS:(c + 1) * NS]).reshape(DEPTH, NS, NMEM, D)
        m["cache_mem_v"] = f(inp["cache_mem_v"][:, c * NS:(c + 1) * NS]).reshape(DEPTH, NS, NMEM, D)
        in_maps.append(m)
    res = run_bass_kernel_spmd(nc, in_maps, core_ids=list(range(NCORES)))
    R = res.results
    cat0 = lambda k: np.concatenate([np.asarray(r[k], np.float32) for r in R], axis=0)
    cat1 = lambda k: np.concatenate([np.asarray(r[k], np.float32) for r in R], axis=1)
    BP = NP * NCORES
    return (cat0("y_prompt"), cat0("y_sample"), cat1("hg_p"), cat1("conv_p"), cat1("pool_p"),
            cat1("mk_p").reshape(DEPTH, BP, NMEM, 4, 256), cat1("mv_p").reshape(DEPTH, BP, NMEM, 4, 256),
            cat1("hg_s"), cat1("conv_s"), cat1("pool_s"))


def kernel(**inputs):
    cfg = dict(CFG)
    return run(cfg, inputs)
```

```python
import numpy as np
import ml_dtypes
import concourse.bass as bass
import concourse.mybir as mybir
from concourse.bass_utils import run_bass_kernel_spmd

F32 = mybir.dt.float32
BF = mybir.dt.bfloat16
AF = mybir.ActivationFunctionType
ALU = mybir.AluOpType
AX = mybir.AxisListType

CFG = dict(DEPTH=4, SEQ=2048, NP=2, NS=1, DEC_SEQ=64, NCORES=8, PAST=2048)
D = 1024
NMEM = 256
EPS = 1e-6
TT = 512
NSLOT = 4
PIECES = ["in5", "in4", "in1", "in3", "in0", "in2", "out0", "out1", "q0", "q1", "o0", "o1",
          "u0", "u1", "u2", "u3", "d00", "d10", "d01", "d11",
          "u4", "u5", "u6", "u7", "d20", "d30", "d21", "d31", "k0", "k1", "v0", "v1"]
NPL = len(PIECES)
PIDX = {n: i for i, n in enumerate(PIECES)}


class Buf:
    __slots__ = ("name", "w", "r", "init", "excl")

    def __init__(self, name, init=(), excl=False):
        self.name = name
        self.excl = excl
        self.w = None
        self.r = {}
        self.init = tuple(init)


class Op:
    __slots__ = ("eng", "fn", "deps", "sig", "signo", "dma", "dmaidx", "id", "label")


class Prog:
    def __init__(self):
        self.ops = []
        self.last_dma = {}
        self.dma_count = {}
        self.last_eng = {}
        self.cur = ""

    def add(self, eng, fn, r=(), w=(), dma=None):
        o = Op()
        o.eng, o.fn, o.dma, o.sig, o.signo, o.dmaidx = eng, fn, dma, False, 0, 0
        o.id = len(self.ops)
        o.label = self.cur
        deps = set()
        mykey = eng if dma is None else "dma:" + dma
        for b in r:
            if b.w is not None:
                deps.add(b.w)
            deps.update(b.init)
            if b.excl:
                deps.update(o2 for k2, o2 in b.r.items() if k2 != mykey)
        for b in w:
            if b.w is not None:
                deps.add(b.w)
            deps.update(b.r.values())
            deps.update(b.init)
        if dma is not None:
            prev = self.last_dma.get(dma)
            if prev is not None:
                deps.add(prev)
            self.last_dma[dma] = o
            o.dmaidx = self.dma_count.get(dma, 0) + 1
            self.dma_count[dma] = o.dmaidx
        deps.discard(o)
        if dma is None and eng == "pe":
            deps = {d for d in deps if not (d.dma is None and d.eng == "pe")}
        o.deps = deps
        for d in deps:
            d.sig = True
        key = eng if dma is None else "dma:" + dma
        for b in r:
            b.r[key] = o
        for b in w:
            b.w = o
            b.r = {}
            b.init = ()
        self.ops.append(o)
        if dma is None:
            self.last_eng[eng] = o
        return o

    def fence(self):
        return list(self.last_eng.values()) + list(self.last_dma.values())


def build(cfg):
    DEPTH, SEQ, NP, NS, DSEQ = cfg["DEPTH"], cfg["SEQ"], cfg["NP"], cfg["NS"], cfg["DEC_SEQ"]
    nc = bass.Bass("TRN2", target_bir_lowering=False)
    P = Prog()

    def din(name, shape, dt=F32):
        return nc.dram_tensor(name, list(shape), dt, kind="ExternalInput").ap()

    def dout(name, shape):
        return nc.dram_tensor(name, list(shape), F32, kind="ExternalOutput").ap()

    x_p = din("x_prompt", [NP, SEQ, D])
    x_s = din("x_sample", [NS, DSEQ, D])
    mem_p = din("mem_prompt", [NP, NMEM, D])
    st_hg = din("state_hgrn", [DEPTH, NS, 4, 128, 128])
    c_conv = din("cache_conv", [DEPTH, NS, 30, 256])
    c_pool = din("cache_pool", [DEPTH, NS, 15, 256])
    c_mk = din("cache_mem_k", [DEPTH, NS, NMEM, D])
    c_mv = din("cache_mem_v", [DEPTH, NS, NMEM, D])
    w_in = din("w_in", [DEPTH, D, 2816])
    w_out = din("w_out", [DEPTH, D, D])
    xq_w = din("xq_w", [DEPTH, D, D])
    xk_w = din("xk_w", [DEPTH, D, D])
    xv_w = din("xv_w", [DEPTH, D, D])
    xo_w = din("xo_w", [DEPTH, D, D])
    w_up = din("w_up", [DEPTH, D, 4 * D])
    w_down = din("w_down", [DEPTH, 4 * D, D])
    NPP = cfg["NPP"]
    OFF = cfg["OFF"]
    pp_d = din("pp", [128, NPP])
    pwbd_d = din("pwbd", [128, DEPTH, 2, 128])
    cst_f = din("cst_f", [128, 256 + TT])
    cst_b = din("cst_b", [128, 128 * 3 + 64], BF)

    y_p = dout("y_prompt", [NP, SEQ, D])
    y_s = dout("y_sample", [NS, DSEQ, D])
    hg_p = dout("hg_p", [DEPTH, NP, 4, 128, 128])
    conv_p = dout("conv_p", [DEPTH, NP, 30, 256])
    pool_p = dout("pool_p", [DEPTH, NP, 15, 256])
    mk_p = dout("mk_p", [DEPTH, NP, NMEM, D])
    mv_p = dout("mv_p", [DEPTH, NP, NMEM, D])
    hg_s = dout("hg_s", [DEPTH, NS, 4, 128, 128])
    conv_s = dout("conv_s", [DEPTH, NS, 30, 256])
    pool_s = dout("pool_s", [DEPTH, NS, 15, 256])
    wscr = nc.dram_tensor("wscr", [DEPTH * NPL, 128, 8 * 512], BF).ap()
    wscr_b = [Buf("wscr%d" % i) for i in range(DEPTH * NPL)]

    class SB:
        top = 16384 + 128
        cnt = 0

    def salloc(shape, dt, at=None):
        nbytes = int(np.prod(shape)) * (4 if dt == F32 else 2)
        nbytes = (nbytes + 63) // 64 * 64
        if at is None:
            off = SB.top
            SB.top += nbytes
            assert SB.top <= 229376 - 64, ("SBUF overflow", SB.top)
        else:
            off = at
        SB.cnt += 1
        t = nc.alloc_sbuf_tensor_at("t%d" % SB.cnt, [128] + list(shape), dt, offset=off)
        return t.ap(), off + nbytes

    def perm(shape, dt, name):
        ap, _ = salloc(shape, dt)
        return ap, Buf(name)

    xT, xT_b = perm([8, TT], F32, "xT")
    xTc = [Buf("xT%d" % c) for c in range(8)]
    hT, _ = perm([8, TT], BF, "hT")
    hTc = [Buf("hT%d" % c) for c in range(8)]
    rstd, rstd_b = perm([TT], F32, "rstd")
    mixT, _ = perm([8, TT], BF, "mixT")
    mix_b = [Buf("mix%d" % c) for c in range(8)]
    sqs = hT
    slots = [perm([8, 512], BF, "slot%d" % i) for i in range(NSLOT)]
    mkT, _ = perm([DEPTH, 8, NMEM], BF, "mkT")
    mvS, _ = perm([DEPTH, 2, D], BF, "mv")
    mk_b = [Buf("mk%d" % l) for l in range(DEPTH)]
    mv_b = [Buf("mv%d" % l) for l in range(DEPTH)]
    Sst, _ = perm([DEPTH, 4, 128], F32, "S")
    S_b = [[Buf("S%d_%d" % (l, h)) for h in range(4)] for l in range(DEPTH)]
    chist, _ = perm([DEPTH, 2, 30], F32, "chist")
    phist, _ = perm([DEPTH, 2, 15], F32, "phist")
    ch_b = [Buf("ch%d" % l) for l in range(DEPTH)]
    ph_b = [Buf("ph%d" % l) for l in range(DEPTH)]
    ppS, pp_b = perm([NPP], F32, "pp")
    pwbd, pwbd_b = perm([DEPTH, 2, 128], BF, "pwbd")
    cf, cf_b = perm([256 + TT], F32, "cf")
    cb, cb_b = perm([128 * 3 + 64], BF, "cb")
    lbv, lb_b = perm([3, DEPTH, 4], F32, "lbv")
    lbt, lbt_b = perm([3, DEPTH, 4], F32, "lbt")
    identf = cf[:, 0:128]
    rmask = cf[:, 128:128 + TT]
    ones256f = cf[:, 128 + TT:256 + TT]
    identb = cb[:, 0:128]
    onesb = cb[:, 128:256]
    ones256 = cb[:, 256:384]
    cmask = cb[0:64, 384:448]
    ARENA0 = SB.top
    ARENA_END = 229376 - 64

    class AR:
        top = ARENA0
        init = ()

    def arena_reset():
        AR.top = ARENA0
        AR.init = tuple(P.fence())

    def aalloc(shape, dt, name):
        ap, end = salloc(shape, dt, at=AR.top)
        AR.top = end
        assert AR.top <= ARENA_END, ("arena overflow", name, AR.top - ARENA0)
        return ap, Buf(name, AR.init)

    ps_t = nc.alloc_psum_tensor("ps", [128, 8, 512], F32)
    ps = ps_t.ap()
    pb = [Buf("bank%d" % i, excl=True) for i in range(8)]

    class PSR:
        lo = 0
        hi = 0
        all = 0

    def bank_lo():
        b = PSR.lo % 4
        PSR.lo += 1
        return b

    def bank8():
        b = PSR.all % 8
        PSR.all += 1
        return b

    def bank_hi():
        b = 4 + PSR.hi % 4
        PSR.hi += 1
        return b

    def MM(out, lhsT, rhs, start, stop, r, w):
        P.add("pe", lambda e: e.matmul(out, lhsT, rhs, start=start, stop=stop), r=r, w=w)

    def TR(out, in_, ident, r, w):
        P.add("pe", lambda e: e.transpose(out, in_, ident), r=r, w=w)

    def ACT(out, in_, func, r, w, bias=0.0, scale=1.0):
        P.add("act", lambda e: e.activation(out, in_, func, bias=bias, scale=scale), r=r, w=w)

    def TT_(eng, out, in0, in1, op, r, w):
        P.add(eng, lambda e: e.tensor_tensor(out, in0, in1, op), r=r, w=w)

    def TS(eng, out, in0, s1, s2, op0, op1, r, w):
        if s2 is None:
            P.add(eng, lambda e: e.tensor_scalar(out, in0, s1, None, op0), r=r, w=w)
        else:
            P.add(eng, lambda e: e.tensor_scalar(out, in0, s1, s2, op0, op1), r=r, w=w)

    def STT(eng, out, in0, sc, in1, op0, op1, r, w):
        P.add(eng, lambda e: e.scalar_tensor_tensor(out, in0, sc, in1, op0, op1), r=r, w=w)

    def CP(eng, out, in_, r, w):
        P.add(eng, lambda e: e.tensor_copy(out, in_), r=r, w=w)

    def MSET(eng, ap, val, w):
        P.add(eng, lambda e: e.memset(ap, val), w=w)

    ioring = [0]

    def DMA(out, in_, r, w, eng="sp", grp=None):
        if grp is None:
            grp = "io%d" % (ioring[0] % 8)
            ioring[0] += 1
        P.add(eng, lambda e: e.dma_start(out=out, in_=in_), r=r, w=w, dma=grp)

    def pcol(name, l=0, j=0):
        o = OFF[name] + j + (0 if name in ("final_g", "invw", "rc16", "eps") else l * cfg["PPL"])
        return ppS[:, o:o + 1]

    DMA(ppS, pp_d, r=[], w=[pp_b])
    DMA(cf, cst_f, r=[], w=[cf_b])
    DMA(cb, cst_b, r=[], w=[cb_b])
    DMA(pwbd, pwbd_d, r=[], w=[pwbd_b], eng="pool", grp="cast0")

    def wsrc(l, name):
        if name.startswith("in"):
            j = int(name[2])
            wd = 512 if j < 5 else 256
            return w_in[l].rearrange("(kc p) n -> p kc n", p=128)[:, :, j * 512:j * 512 + wd], wd
        if name.startswith("out"):
            src = w_out
        elif name[0] == "q":
            src = xq_w
        elif name[0] == "o":
            src = xo_w
        elif name[0] == "k":
            src = xk_w
        elif name[0] == "v":
            src = xv_w
        elif name[0] == "u":
            j = int(name[1])
            return w_up[l].rearrange("(kc p) n -> p kc n", p=128)[:, :, j * 512:(j + 1) * 512], 512
        elif name[0] == "d":
            kg, half = int(name[1]), int(name[2])
            return (w_down[l][kg * 1024:(kg + 1) * 1024, :].rearrange("(kc p) n -> p kc n", p=128)
                    [:, :, half * 512:(half + 1) * 512]), 512
        j = int(name[-1])
        return src[l].rearrange("(kc p) n -> p kc n", p=128)[:, :, j * 512:(j + 1) * 512], 512

    def cast_all(order):
        castn = 0
        seen = set()
        for (l, name) in order:
            if (l, name) in seen:
                continue
            seen.add((l, name))
            i = l * NPL + PIDX[name]
            src, wd = wsrc(l, name)
            dst = wscr[i].rearrange("p (kc n) -> p kc n", n=512)[:, :, 0:wd]
            DMA(dst, src, r=[], w=[wscr_b[i]], eng="pool", grp="cast%d" % (castn % 8))
            castn += 1

    lbp = ppS[:, OFF["lbp"]:OFF["lbp"] + DEPTH * 4].rearrange("p (l h) -> p l h", h=4)
    ex = lbt[:, 0]
    ACT(ex, lbp, AF.Exp, r=[pp_b], w=[lbt_b])
    den = lbt[:, 1, 0]
    CP("dve", den, ex[:, 0], r=[lbt_b], w=[lbt_b])
    for l in range(1, DEPTH):
        TT_("dve", den, den, ex[:, l], ALU.add, r=[lbt_b], w=[lbt_b])
    P.add("dve", lambda e: e.reciprocal(den, den), r=[lbt_b], w=[lbt_b])
    MSET("dve", lbv[:, 0, 0], 0.0, w=[lb_b])
    for l in range(1, DEPTH):
        TT_("dve", lbt[:, 2, 0], ex[:, l], den, ALU.mult, r=[lbt_b], w=[lbt_b])
        TT_("dve", lbv[:, 0, l], lbv[:, 0, l - 1], lbt[:, 2, 0], ALU.add, r=[lbt_b, lb_b], w=[lb_b])
    TS("dve", lbv[:, 1], lbv[:, 0], -1.0, 1.0, ALU.mult, ALU.add, r=[lb_b], w=[lb_b])
    TS("dve", lbv[:, 2], lbv[:, 1], -1.0, None, ALU.mult, None, r=[lb_b], w=[lb_b])

    seqs = [("p", b) for b in range(NP)] + [("s", b) for b in range(NS)]

    def piece_plan():
        plan = []
        for kind, b in seqs:
            T = TT if kind == "p" else DSEQ
            ntile = (SEQ // TT) if kind == "p" else 1
            if kind == "p":
                for l in range(DEPTH):
                    plan += [(l, n) for n in ("k0", "k1", "v0", "v1")]
            for t in range(ntile):
                for l in range(DEPTH):
                    plan += [(l, n) for n in PIECES[:28]]
        return plan

    plan = piece_plan()
    U = []
    for pc in plan:
        if pc not in U:
            U.append(pc)
    upos = {pc: i for i, pc in enumerate(U)}

    class CQ:
        done = 0

    def ensure_cast(upto):
        while CQ.done < min(upto + 1, len(U)):
            l, name = U[CQ.done]
            i = l * NPL + PIDX[name]
            src, wd = wsrc(l, name)
            dst = wscr[i].rearrange("p (kc n) -> p kc n", n=512)[:, :, 0:wd]
            DMA(dst, src, r=[], w=[wscr_b[i]], eng="pool", grp="cast%d" % (CQ.done % 8))
            CQ.done += 1

    class WQ:
        nxt = 0
        issued = 0

    def wget(l, name):
        i = WQ.nxt
        assert plan[i] == (l, name), (plan[i], l, name)
        while WQ.issued < len(plan) and WQ.issued < i + NSLOT:
            j = WQ.issued
            pl, pn = plan[j]
            gi = pl * NPL + PIDX[pn]
            ensure_cast(upos[(pl, pn)] + 6)
            sap, sbuf = slots[j % NSLOT]
            if pn == "in5":
                DMA(sap[:, :, 0:256], wscr[gi].rearrange("p (kc n) -> p kc n", n=512)[:, :, 0:256],
                    r=[wscr_b[gi]], w=[sbuf], grp="w%d" % (j % NSLOT))
            else:
                DMA(sap.rearrange("p kc n -> p (kc n)"), wscr[gi], r=[wscr_b[gi]], w=[sbuf], grp="w%d" % (j % NSLOT))
            WQ.issued += 1
        WQ.nxt += 1
        return slots[i % NSLOT]

    class NS:
        bank = None
        pend = []
        cnt = 0

    def rms_finish(T, bk, inv_n):
        ACT(rstd[:, 0:T], ps[:, bk, 0:T], AF.Ln, r=[pb[bk], pp_b], w=[rstd_b], bias=pcol("eps"), scale=inv_n)
        ACT(rstd[:, 0:T], rstd[:, 0:T], AF.Exp, r=[rstd_b], w=[rstd_b], scale=-0.5)

    def rms_stats_bulk(T):
        ACT(sqs[:, :, 0:T], xT[:, :, 0:T], AF.Square, r=xTc, w=hTc)
        bk = bank_lo()
        for c in range(8):
            MM(ps[:, bk, 0:T], onesb, sqs[:, c, 0:T], c == 0, c == 7, r=[hTc[c], cb_b], w=[pb[bk]])
        return bk

    def norm_begin(bk):
        NS.bank, NS.pend, NS.cnt = bk, [], 0

    def norm_push(T, c):
        ACT(sqs[:, c, 0:T], xT[:, c, 0:T], AF.Square, r=[xTc[c]], w=[hTc[c]])
        NS.pend.append(c)
        while len(NS.pend) > 2:
            norm_mm(T, NS.pend.pop(0))

    def norm_mm(T, c):
        MM(ps[:, NS.bank, 0:T], onesb, sqs[:, c, 0:T], NS.cnt == 0, NS.cnt == 7, r=[hTc[c], cb_b], w=[pb[NS.bank]])
        NS.cnt += 1

    def norm_flush(T):
        while NS.pend:
            norm_mm(T, NS.pend.pop(0))
        assert NS.cnt == 8
        bk = NS.bank
        NS.bank = None
        return bk

    def norm_to_hT(T, l, gname, bk=None):
        if bk is None:
            bk = rms_stats_bulk(T)
        rms_finish(T, bk, 1.0 / D)
        for c in range(8):
            STT("dve", hT[:, c, 0:T], xT[:, c, 0:T], pcol(gname, l, c), rstd[:, 0:T],
                ALU.mult, ALU.mult, r=[xTc[c], rstd_b, pp_b], w=[hTc[c]])

    def proj_fm(T, l, name, nch, rhsT, rhs_b, consume, banks=None, kc_order=None):
        sap, sbuf = wget(l, name)

        def nextbank():
            if banks is None:
                return bank8()
            bk = banks[PSR.all % len(banks)]
            PSR.all += 1
            return bk
        if kc_order is not None:
            bks = [nextbank() for m in range(nch)]
            for i, kc in enumerate(kc_order):
                for m in range(nch):
                    MM(ps[:, bks[m], 0:T], sap[:, kc, m * 128:(m + 1) * 128], rhsT[:, kc, 0:T], i == 0, i == 7,
                       r=[sbuf, rhs_b[kc]], w=[pb[bks[m]]])
            for m in range(nch):
                consume(m, bks[m])
            return
        for m in range(nch):
            bk = nextbank()
            for kc in range(8):
                MM(ps[:, bk, 0:T], sap[:, kc, m * 128:(m + 1) * 128], rhsT[:, kc, 0:T], kc == 0, kc == 7,
                   r=[sbuf, rhs_b[kc]], w=[pb[bk]])
            consume(m, bk)

    def add_to_x(T, base):
        def f(m, bk):
            c = base + m
            TT_("dve", xT[:, c, 0:T], xT[:, c, 0:T], ps[:, bk, 0:T], ALU.add, r=[pb[bk], xTc[c]], w=[xTc[c]])
            if NS.bank is not None:
                norm_push(T, c)
        return f

    def layer(T, l, first_tile, nbk_in=None):
        NCH = T // 64
        L30 = 30 + T
        L15 = 15 + T
        arena_reset()
        vS, v_b = aalloc([8, 512], BF, "v")
        qp, qp_b = aalloc([4, TT], BF, "qp")
        kp, kp_b = aalloc([4, TT], BF, "kp")
        sog, sog_b = aalloc([4, TT], BF, "sog")
        S0b = [[aalloc([128], BF, "S0b%d_%d" % (h, i)) for i in range(2)] for h in range(4)]
        E1 = [aalloc([TT], F32, "E1_%d" % h) for h in range(4)]
        csg = [aalloc([8], F32, "cs%d" % h) for h in range(4)]
        dpool, dp_b = aalloc([2, TT], BF, "dpool")
        fc32, fc32_b = aalloc([2, 30 + TT], F32, "fc32")
        fcb, fcb_b = aalloc([2, 30 + TT], BF, "fcb")
        diag, diag_b = aalloc([2, 31, 128], BF, "diag")
        sg, sg_b = aalloc([2, TT], F32, "sg")
        dwb32, dwb32_b = aalloc([2, TT], F32, "dwb32")
        dsq, dsq_b = aalloc([2, TT], BF, "dsq")
        ex2s, ex2s_b = aalloc([TT], F32, "ex2s")
        mean_s, mean_sb = sg[:, 0, :], sg_b
        rs, rs_b = sg[:, 1, :], sg_b
        TMP0 = AR.top
        P.cur = "norm_mix"
        norm_to_hT(T, l, "g_mix", nbk_in)

        P.cur = "pool"
        fullp, fullp_b = aalloc([2, 15 + TT], F32, "fullp")
        s2, s2_b = aalloc([2, 15 + TT], F32, "s2")
        s4, s4_b = aalloc([15 + TT], F32, "s4")
        s8, s8_b = aalloc([15 + TT], F32, "s8")
        wsum, ws_b = aalloc([2, TT], F32, "ws")
        CP("pool", fullp[:, :, 0:15], phist[:, l], r=[ph_b[l]], w=[fullp_b])
        CP("pool", fc32[:, :, 0:30], chist[:, l], r=[ch_b[l]], w=[fc32_b])
        o = OFF["conv_w"] + l * cfg["PPL"]
        TT_("dve", diag.rearrange("p c j k -> p (c j) k"),
            identb.rearrange("p (o k) -> p o k", o=1).broadcast_to([128, 62, 128]),
            ppS[:, o:o + 62].rearrange("p (j o) -> p j o", o=1).broadcast_to([128, 62, 128]), ALU.mult,
            r=[cb_b, pp_b], w=[diag_b])

        def c_pu(m, bk):
            ACT(fullp[:, m, 15:L15], ps[:, bk, 0:T], AF.Copy, r=[pb[bk]], w=[fullp_b])
        proj_fm(T, l, "in5", 2, hT, hTc, c_pu, kc_order=list(range(8)))
        TT_("pool", s2[:, :, 1:L15], fullp[:, :, 1:L15], fullp[:, :, 0:L15 - 1], ALU.add, r=[fullp_b], w=[s2_b])
        TT_("pool", s4[:, 3:L15], s2[:, 1, 3:L15], s2[:, 1, 1:L15 - 2], ALU.add, r=[s2_b], w=[s4_b])
        TT_("pool", s8[64:128, 7:L15], s4[64:128, 7:L15], s4[64:128, 3:L15 - 4], ALU.add, r=[s4_b], w=[s8_b])
        TT_("pool", wsum[0:64, 0, 0:T], fullp[0:64, 0, 15:L15], fullp[0:64, 0, 14:L15 - 1], ALU.add,
            r=[fullp_b], w=[ws_b])
        TT_("pool", wsum[64:128, 0, 0:T], s2[64:128, 0, 15:L15], s2[64:128, 0, 13:L15 - 2], ALU.add,
            r=[s2_b], w=[ws_b])
        TT_("pool", wsum[0:64, 1, 0:T], s4[0:64, 15:L15], s4[0:64, 11:L15 - 4], ALU.add, r=[s4_b], w=[ws_b])
        TT_("pool", wsum[64:128, 1, 0:T], s8[64:128, 15:L15], s8[64:128, 7:L15 - 8], ALU.add, r=[s8_b], w=[ws_b])
        CP("pool", phist[:, l], fullp[:, :, T:T + 15], r=[fullp_b], w=[ph_b[l]])
        if first_tile:
            rc = ppS[:, OFF["rc16"]:OFF["rc16"] + 32].rearrange("p (c t) -> p c t", t=16)
            TT_("pool", wsum[:, :, 0:16], wsum[:, :, 0:16], rc, ALU.mult, r=[ws_b, pp_b], w=[ws_b])

        P.cur = "conv"
        cab = [None, None]

        def c_cacg(m, bk):
            if m < 2:
                cab[m] = bk
            else:
                ch = m - 2
                ACT(sg[:, ch, 0:T], ps[:, bk, 0:T], AF.Sigmoid, r=[pb[bk]], w=[sg_b])
                TT_("dve", fc32[:, ch, 30:L30], ps[:, cab[ch], 0:T], sg[:, ch, 0:T], ALU.mult,
                    r=[pb[cab[ch]], sg_b], w=[fc32_b])
        proj_fm(T, l, "in4", 4, hT, hTc, c_cacg)
        CP("pool", fcb[:, :, 0:L30], fc32[:, :, 0:L30], r=[fc32_b], w=[fcb_b])
        CP("pool", chist[:, l], fc32[:, :, T:T + 30], r=[fc32_b], w=[ch_b[l]])
        P.cur = "pool"
        for ch in range(2):
            STT("dve", dpool[:, ch, 0:T], wsum[:, ch, 0:T], pcol("invw", 0, ch), fullp[:, ch, 15:L15],
                ALU.mult, ALU.subtract, r=[ws_b, fullp_b, pp_b], w=[dp_b])

        P.cur = "hgrn_prep"
        AR.top = TMP0
        AR.init = tuple(P.fence())
        sig = [aalloc([TT], F32, "sig%d" % i) for i in range(4)]
        sq_ = [aalloc([TT], F32, "sq_%d" % i) for i in range(4)]
        lgf = [aalloc([TT], F32, "lgf%d" % i) for i in range(2)]
        Acs = [aalloc([TT], F32, "A%d" % i) for i in range(2)]
        gref = [aalloc([8], F32, "gref%d" % i) for i in range(2)]

        def c_f(h, bk):
            ACT(sig[h][0][:, 0:T], ps[:, bk, 0:T], AF.Sigmoid, r=[pb[bk]], w=[sig[h][1]])
        proj_fm(T, l, "in1", 4, hT, hTc, c_f)

        def chain_f(h):
            i = h % 2
            (sg_, sgb), (lg_, lgb), (A_, Ab) = sig[h], lgf[i], Acs[i]
            e1, e1b = E1[h]
            gr, grb = gref[i]
            cs, csb = csg[h]
            ACT(lg_[:, 0:T], sg_[:, 0:T], AF.Ln, r=[sgb, lb_b], w=[lgb], bias=lbv[:, 0, l, h:h + 1],
                scale=lbv[:, 1, l, h:h + 1])
            P.add("dve", lambda e: e.tensor_tensor_scan(A_[:, 0:T], rmask[:, 0:T], lg_[:, 0:T], 0.0, ALU.mult, ALU.add),
                  r=[lgb, cf_b], w=[Ab])
            A3 = A_[:, 0:T].rearrange("p (c t) -> p c t", t=64)
            TT_("dve", lg_[:, 0:T].rearrange("p (c t) -> p c t", t=64), A3,
                A3[:, :, 31:32].broadcast_to([128, NCH, 64]), ALU.subtract, r=[Ab], w=[lgb])
            ACT(gr[:, 0:NCH], A3[:, :, 31], AF.Exp, r=[Ab], w=[grb])
            ACT(e1[:, 0:T], lg_[:, 0:T], AF.Exp, r=[lgb], w=[e1b])
            ACT(A_[:, 0:T], lg_[:, 0:T], AF.Exp, r=[lgb], w=[Ab], scale=-1.0)
            TS("dve", sg_[:, 0:T], sg_[:, 0:T], lbv[:, 2, l, h:h + 1], lbv[:, 1, l, h:h + 1], ALU.mult, ALU.add,
               r=[sgb, lb_b], w=[sgb])
            TT_("pool", kp[:, h, 0:T], sg_[:, 0:T], A_[:, 0:T], ALU.mult, r=[sgb, Ab], w=[kp_b])
            CP("dve", cs[:, 0:1], gr[:, 0:1], r=[grb], w=[csb])
            if NCH > 1:
                e13 = e1[:, 0:T].rearrange("p (c t) -> p c t", t=64)
                TT_("dve", cs[:, 1:NCH], gr[:, 1:NCH], e13[:, 0:NCH - 1, 63], ALU.mult, r=[grb, e1b], w=[csb])

        def c_og(h, bk):
            ACT(sog[:, h, 0:T], ps[:, bk, 0:T], AF.Silu, r=[pb[bk]], w=[sog_b])
        proj_fm(T, l, "in3", 4, hT, hTc, c_og)

        def c_q(h, bk):
            ACT(sq_[h][0][:, 0:T], ps[:, bk, 0:T], AF.Silu, r=[pb[bk]], w=[sq_[h][1]])
        proj_fm(T, l, "in0", 4, hT, hTc, c_q)
        for h in range(4):
            chain_f(h)
        P.cur = "pool"
        for ch in range(2):
            bk = bank8()
            MM(ps[:, bk, 0:T], pwbd[:, l, ch, :], dpool[:, ch, 0:T], True, True, r=[dp_b, pwbd_b], w=[pb[bk]])
            ACT(mixT[:, 6 + ch, 0:T], ps[:, bk, 0:T], AF.Copy, r=[pb[bk], pp_b], w=[mix_b[6 + ch]],
                scale=pcol("pool_scale", l, ch))
        P.cur = "hgrn_prep"

        for h in range(4):
            TT_("dve", qp[:, h, 0:T], sq_[h][0][:, 0:T], E1[h][0][:, 0:T], ALU.mult,
                r=[sq_[h][1], E1[h][1]], w=[qp_b])
        sap, sbuf = wget(l, "in2")
        for c in range(NCH):
            bk = bank8()
            for kc in range(8):
                MM(ps[0:64, bk, 0:512], hT[:, kc, c * 64:(c + 1) * 64], sap[:, kc, :], kc == 0, kc == 7,
                   r=[sbuf, hTc[kc]], w=[pb[bk]])
            if c % 2 == 0:
                ACT(vS[0:64, c, :], ps[0:64, bk, 0:512], AF.Copy, r=[pb[bk]], w=[v_b])
            else:
                CP("dve", vS[0:64, c, :], ps[0:64, bk, 0:512], r=[pb[bk]], w=[v_b])

        conv_tasks = []

        def conv_mm(ch, j):
            def f():
                MM(ps[:, ch, 0:T], diag[:, ch, j, :], fcb[:, ch, j:j + T], j == 0, j == 30,
                   r=[diag_b, fcb_b], w=[pb[ch]])
            return f

        def conv_tail():
            for ch in range(2):
                ACT(dwb32[:, ch, 0:T], ps[:, ch, 0:T], AF.Identity, r=[pb[ch], pp_b], w=[dwb32_b],
                    bias=pcol("conv_b", l, ch))
                ACT(dsq[:, ch, 0:T], ps[:, ch, 0:T], AF.Square, r=[pb[ch], pp_b], w=[dsq_b],
                    bias=pcol("conv_b", l, ch))

        def conv_stats():
            bm, be = 0, 1
            for ch in range(2):
                MM(ps[:, bm, 0:T], ones256f, dwb32[:, ch, 0:T], ch == 0, ch == 1, r=[dwb32_b, cf_b], w=[pb[bm]])
            for ch in range(2):
                MM(ps[:, be, 0:T], ones256, dsq[:, ch, 0:T], ch == 0, ch == 1, r=[dsq_b, cb_b], w=[pb[be]])
            ACT(mean_s[:, 0:T], ps[:, bm, 0:T], AF.Copy, r=[pb[bm]], w=[mean_sb])
            ACT(ex2s[:, 0:T], ps[:, be, 0:T], AF.Copy, r=[pb[be]], w=[ex2s_b])
            TT_("pool", rs[:, 0:T], mean_s[:, 0:T], mean_s[:, 0:T], ALU.mult, r=[mean_sb], w=[rs_b])
            TT_("pool", rs[:, 0:T], ex2s[:, 0:T], rs[:, 0:T], ALU.subtract, r=[ex2s_b, rs_b], w=[rs_b])
            ACT(rs[:, 0:T], rs[:, 0:T], AF.Ln, r=[rs_b, pp_b], w=[rs_b], bias=pcol("eps"))
            ACT(rs[:, 0:T], rs[:, 0:T], AF.Exp, r=[rs_b], w=[rs_b], scale=-0.5)
            for ch in range(2):
                TT_("pool", dwb32[:, ch, 0:T], dwb32[:, ch, 0:T], mean_s[:, 0:T], ALU.subtract,
                    r=[dwb32_b, mean_sb], w=[dwb32_b])
                TT_("pool", dwb32[:, ch, 0:T], dwb32[:, ch, 0:T], rs[:, 0:T], ALU.mult, r=[dwb32_b, rs_b], w=[dwb32_b])
                ACT(mixT[:, 4 + ch, 0:T], dwb32[:, ch, 0:T], AF.Silu, r=[dwb32_b, pp_b], w=[mix_b[4 + ch]],
                    bias=pcol("conv_ln_b", l, ch), scale=pcol("conv_ln_g", l, ch))
        for ch in range(2):
            for j in range(31):
                conv_tasks.append(conv_mm(ch, j))
        conv_tasks.append(conv_tail)
        for _ in range(12):
            conv_tasks.append(None)
        conv_tasks.append(conv_stats)

        def conv_step(n):
            for _ in range(n):
                if conv_tasks:
                    t_ = conv_tasks.pop(0)
                    if t_ is not None:
                        P.cur = "conv"
                        t_()
                        P.cur = "hgrn_chunks"

        P.cur = "hgrn_chunks"
        AR.top = TMP0
        AR.init = tuple(P.fence())
        scm, scm_b = aalloc([4, 8, 64], BF, "scm")
        ktm, ktm_b = aalloc([4, 8, 128], BF, "ktm")
        osq, osq_b = aalloc([2, TT], BF, "osq")
        ro, ro_b = aalloc([2, TT], F32, "ro")
        otmp, otmp_b = aalloc([2, TT], F32, "otmp")
        for h in range(4):
            bA = 2
            for c in range(NCH):
                cs_ = slice(c * 64, (c + 1) * 64)
                MM(ps[0:64, bA, cs_], kp[:, h, cs_], qp[:, h, cs_], True, True, r=[kp_b, qp_b], w=[pb[bA]])
            TT_("dve", scm[0:64, h, 0:NCH, :], ps[0:64, bA, 0:T].rearrange("p (c t) -> p c t", t=64),
                cmask.rearrange("p (o t) -> p o t", o=1).broadcast_to([64, NCH, 64]), ALU.mult,
                r=[pb[bA], cb_b], w=[scm_b])
            bB = 3
            psb = ps[:, bB, :].bitcast(BF)
            conv_step(6)
            for c in range(NCH):
                TR(psb[0:64, c * 128:(c + 1) * 128], kp[:, h, c * 64:(c + 1) * 64], identb,
                   r=[kp_b, cb_b], w=[pb[bB]])
            ACT(ktm[0:64, h, 0:NCH, :], psb[0:64, 0:NCH * 128].rearrange("p (c d) -> p c d", d=128), AF.Copy,
                r=[pb[bB]], w=[ktm_b])
        ob = [4, 5, 6, 7]

        def emit_dS(c):
            bk = 2 + c % 2
            for h in range(4):
                MM(ps[:, bk, h * 128:(h + 1) * 128], ktm[0:64, h, c, :], vS[0:64, c, h * 128:(h + 1) * 128],
                   True, True, r=[ktm_b, v_b], w=[pb[bk]])
        emit_dS(0)
        for h in range(4):
            s0, s0b = S0b[h][0]
            TS("dve", s0, Sst[:, l, h, :], csg[h][0][:, 0:1], None, ALU.mult, None,
               r=[S_b[l][h], csg[h][1]], w=[s0b])
        for c in range(NCH):
            cs_ = slice(c * 64, (c + 1) * 64)
            if c + 1 < NCH:
                emit_dS(c + 1)
            for h in range(4):
                s0, s0b = S0b[h][c % 2]
                cs, csb = csg[h]
                Sh = Sst[:, l, h, :]
                vh = vS[0:64, c, h * 128:(h + 1) * 128]
                MM(ps[:, ob[h], cs_], vh, scm[0:64, h, c, :], True, False, r=[v_b, scm_b], w=[pb[ob[h]]])
                MM(ps[:, ob[h], cs_], s0, qp[:, h, cs_], False, True, r=[s0b, qp_b], w=[pb[ob[h]]])
                b3 = 2 + c % 2
                STT("dve", Sh, Sh, cs[:, c:c + 1], ps[:, b3, h * 128:(h + 1) * 128], ALU.mult, ALU.add,
                    r=[S_b[l][h], csb, pb[b3]], w=[S_b[l][h]])
                if c + 1 < NCH:
                    s1, s1b = S0b[h][(c + 1) % 2]
                    TS("dve", s1, Sh, cs[:, c + 1:c + 2], None, ALU.mult, None, r=[S_b[l][h], csb], w=[s1b])
                conv_step(4)
        conv_step(1000)
        P.cur = "hgrn_out"
        for h in range(4):
            Sh = Sst[:, l, h, :]
            TS("dve", Sh, Sh, E1[h][0][:, T - 1:T], None, ALU.mult, None, r=[S_b[l][h], E1[h][1]], w=[S_b[l][h]])
        for g in range(2):
            o0 = 4 + 2 * g
            ACT(osq[:, :, 0:T], ps[:, o0:o0 + 2, 0:T], AF.Square, r=[pb[o0], pb[o0 + 1]], w=[osq_b])
            n0 = 2
            for j in range(2):
                MM(ps[:, n0 + j, 0:T], onesb, osq[:, j, 0:T], True, True, r=[osq_b, cb_b], w=[pb[n0 + j]])
            for j in range(2):
                h = 2 * g + j
                STT("dve", otmp[:, j, 0:T], ps[:, o0 + j, 0:T], pcol("hg_norm_g", l, h), sog[:, h, 0:T],
                    ALU.mult, ALU.mult, r=[pb[o0 + j], sog_b, pp_b], w=[otmp_b])
            ACT(ro[:, :, 0:T], ps[:, n0:n0 + 2, 0:T], AF.Ln, r=[pb[n0], pb[n0 + 1], pp_b], w=[ro_b],
                bias=pcol("eps"), scale=1.0 / 128)
            ACT(ro[:, :, 0:T], ro[:, :, 0:T], AF.Exp, r=[ro_b], w=[ro_b], scale=-0.5)
            TT_("dve", mixT[:, 2 * g:2 * g + 2, 0:T], otmp[:, :, 0:T], ro[:, :, 0:T], ALU.mult,
                r=[otmp_b, ro_b], w=[mix_b[2 * g], mix_b[2 * g + 1]])
        P.cur = "w_out"
        norm_begin(7)
        proj_fm(T, l, "out0", 4, mixT, mix_b, add_to_x(T, 0), banks=[0, 1, 2, 3, 4, 5, 6], kc_order=[4, 5, 6, 7, 0, 1, 2, 3])
        proj_fm(T, l, "out1", 4, mixT, mix_b, add_to_x(T, 4), banks=[0, 1, 2, 3, 4, 5, 6])
        nbk = norm_flush(T)

        arena_reset()
        qT, qT_b = aalloc([8, TT], BF, "qT")
        ee = [aalloc([2, TT], BF, "e%d" % i) for i in range(2)]
        rden = [aalloc([TT], F32, "rden%d" % i) for i in range(2)]
        P.cur = "norm_x"
        norm_to_hT(T, l, "g_x", nbk)

        def c_qx(base):
            def f(m, bk):
                ACT(qT[:, base + m, 0:T], ps[:, bk, 0:T], AF.Copy, r=[pb[bk]], w=[qT_b])
            return f
        P.cur = "xq"
        proj_fm(T, l, "q0", 4, hT, hTc, c_qx(0), kc_order=list(range(8)))
        proj_fm(T, l, "q1", 4, hT, hTc, c_qx(4))
        P.cur = "attn"
        for hh in range(4):
            e_, eb = ee[hh % 2]
            rd, rdb = rden[hh % 2]
            for mc in range(2):
                bk = bank_lo()
                for dc in range(2):
                    MM(ps[:, bk, 0:T], mkT[:, l, 2 * hh + dc, mc * 128:(mc + 1) * 128], qT[:, 2 * hh + dc, 0:T],
                       dc == 0, dc == 1, r=[mk_b[l], qT_b], w=[pb[bk]])
                ACT(e_[:, mc, 0:T], ps[:, bk, 0:T], AF.Exp, r=[pb[bk]], w=[eb], scale=1.0 / 16.0)
            bd = bank_hi()
            for mc in range(2):
                MM(ps[:, bd, 0:T], onesb, e_[:, mc, 0:T], mc == 0, mc == 1, r=[eb, cb_b], w=[pb[bd]])
            P.add("dve", lambda e, rd=rd, bd=bd: e.reciprocal(rd[:, 0:T], ps[:, bd, 0:T]), r=[pb[bd]], w=[rdb])
            for dc in range(2):
                bk = bank_lo()
                for mc in range(2):
                    MM(ps[:, bk, 0:T], mvS[:, l, mc, hh * 256 + dc * 128:hh * 256 + (dc + 1) * 128], e_[:, mc, 0:T],
                       mc == 0, mc == 1, r=[mv_b[l], eb], w=[pb[bk]])
                TT_("dve", mixT[:, 2 * hh + dc, 0:T], ps[:, bk, 0:T], rd[:, 0:T], ALU.mult,
                    r=[pb[bk], rdb], w=[mix_b[2 * hh + dc]])
        P.cur = "xo"
        norm_begin(7)
        proj_fm(T, l, "o0", 4, mixT, mix_b, add_to_x(T, 0), banks=[0, 1, 2, 3, 4, 5, 6], kc_order=list(range(8)))
        proj_fm(T, l, "o1", 4, mixT, mix_b, add_to_x(T, 4), banks=[0, 1, 2, 3, 4, 5, 6])
        nbk = norm_flush(T)

        arena_reset()
        hid = [aalloc([16, TT], BF, "hid%d" % i) for i in range(2)]
        rl = [aalloc([TT], BF, "rl%d" % i) for i in range(3)]
        P.cur = "norm_ffn"
        norm_to_hT(T, l, "g_ffn", nbk)
        P.cur = "ffn"
        cnt = [0]
        for half in range(2):
            hd, hdb = hid[half]

            def c_up(j0):
                def f(m, bk):
                    r_, rb_ = rl[cnt[0] % 3]
                    cnt[0] += 1
                    ACT(r_[:, 0:T], ps[:, bk, 0:T], AF.Relu, r=[pb[bk]], w=[rb_])
                    TT_("dve", hd[:, j0 + m, 0:T], r_[:, 0:T], r_[:, 0:T], ALU.mult, r=[rb_], w=[hdb])
                return f
            for j in range(4):
                proj_fm(T, l, "u%d" % (half * 4 + j), 4, hT, hTc, c_up(j * 4),
                        kc_order=(list(range(8)) if (half == 0 and j == 0) else None))
            for colh in range(2):
                if half == 1 and colh == 0:
                    norm_begin(3)
                bks = [bank_hi() for m in range(4)]
                for kg in range(2):
                    sap, sbuf = wget(l, "d%d%d" % (half * 2 + kg, colh))
                    for m in range(4):
                        for kc in range(8):
                            MM(ps[:, bks[m], 0:T], sap[:, kc, m * 128:(m + 1) * 128], hd[:, kg * 8 + kc, 0:T],
                               kg == 0 and kc == 0, kg == 1 and kc == 7, r=[sbuf, hdb], w=[pb[bks[m]]])
                for m in range(4):
                    add_to_x(T, colh * 4)(m, bks[m])
        return norm_flush(T)

    def load_tile(T, src):
        P.cur = "load"
        arena_reset()
        nb = (T + 127) // 128
        tb = min(T, 128)
        xst, xst_b = aalloc([4, D], F32, "xst")
        DMA(xst[0:tb, 0:nb, :], src.rearrange("(b p) f -> p b f", p=tb), r=[], w=[xst_b])
        for c in range(8):
            bk = bank_lo()
            for b in range(nb):
                TR(ps[:, bk, b * 128:b * 128 + tb], xst[0:tb, b, c * 128:(c + 1) * 128], identf[0:tb, 0:tb],
                   r=[xst_b, cf_b], w=[pb[bk]])
            if c % 2 == 0:
                ACT(xT[:, c, 0:T], ps[:, bk, 0:T], AF.Copy, r=[pb[bk]], w=[xTc[c]])
            else:
                CP("dve", xT[:, c, 0:T], ps[:, bk, 0:T], r=[pb[bk]], w=[xTc[c]])

    def store_tile(T, dst, nbk_in=None):
        P.cur = "store"
        arena_reset()
        nb = (T + 127) // 128
        tb = min(T, 128)
        yT, yT_b = aalloc([8, TT], F32, "yT")
        yst, yst_b = aalloc([4, D], F32, "yst")
        rms_finish(T, nbk_in if nbk_in is not None else rms_stats_bulk(T), 1.0 / D)
        for c in range(8):
            STT("dve", yT[:, c, 0:T], xT[:, c, 0:T], pcol("final_g", 0, c), rstd[:, 0:T],
                ALU.mult, ALU.mult, r=[xTc[c], rstd_b, pp_b], w=[yT_b])
        for b in range(nb):
            for hf in range(2):
                bk = bank_lo()
                for cc in range(4):
                    c = hf * 4 + cc
                    TR(ps[0:tb, bk, cc * 128:(cc + 1) * 128], yT[:, c, b * 128:b * 128 + tb], identf,
                       r=[yT_b, cf_b], w=[pb[bk]])
                if hf == 0:
                    ACT(yst[0:tb, b, 0:512], ps[0:tb, bk, :], AF.Copy, r=[pb[bk]], w=[yst_b])
                else:
                    CP("dve", yst[0:tb, b, 512:1024], ps[0:tb, bk, :], r=[pb[bk]], w=[yst_b])
        DMA(dst.rearrange("(b p) f -> p b f", p=tb), yst[0:tb, 0:nb, :], r=[yst_b], w=[])

    def prep_prompt(b):
        P.cur = "prep_p"
        for l in range(DEPTH):
            for h in range(4):
                MSET("pool", Sst[:, l, h, :], 0.0, w=[S_b[l][h]])
            MSET("pool", chist[:, l], 0.0, w=[ch_b[l]])
            MSET("pool", phist[:, l], 0.0, w=[ph_b[l]])
        arena_reset()
        mst, mst_b = aalloc([2, D], F32, "mst")
        junk, junk_b = aalloc([D], F32, "junk")
        ssm, ssm_b = aalloc([2], F32, "ssm")
        memn, memn_b = aalloc([2, D], BF, "memn")
        mnT, mnT_b = aalloc([8, NMEM], BF, "mnT")
        mhTs = [aalloc([8, NMEM], BF, "mhT%d" % i) for i in range(2)]
        kvst = [aalloc([2, D], F32, "kvst%d" % i) for i in range(2)]
        DMA(mst, mem_p[b].rearrange("(mc p) f -> p mc f", p=128), r=[], w=[mst_b])
        if cfg.get("CUT") == 1:
            return
        for mc in range(2):
            ACT(junk, mst[:, mc, :], AF.Square, r=[mst_b], w=[junk_b])
            P.add("dve", lambda e, mc=mc: e.reduce_sum(ssm[:, mc:mc + 1], junk, AX.X), r=[junk_b], w=[ssm_b])
        ACT(ssm, ssm, AF.Ln, r=[ssm_b, pp_b], w=[ssm_b], bias=pcol("eps"), scale=1.0 / D)
        ACT(ssm, ssm, AF.Exp, r=[ssm_b], w=[ssm_b], scale=-0.5)
        for mc in range(2):
            TS("dve", memn[:, mc, :], mst[:, mc, :], ssm[:, mc:mc + 1], None, ALU.mult, None,
               r=[mst_b, ssm_b], w=[memn_b])
        if cfg.get("CUT") == 2:
            return
        for mc in range(2):
            bk = bank_lo()
            psb = ps[:, bk, :].bitcast(BF)
            for c in range(8):
                TR(psb[:, c * 128:(c + 1) * 128], memn[:, mc, c * 128:(c + 1) * 128], identb,
                   r=[memn_b, cb_b], w=[pb[bk]])
            CP("dve", mnT[:, :, mc * 128:(mc + 1) * 128], psb.rearrange("p (c m) -> p c m", m=128),
               r=[pb[bk]], w=[mnT_b])
        if cfg.get("CUT") == 3:
            return
        for l in range(DEPTH):
            mhT, mhT_b = mhTs[l % 2]
            for c in range(8):
                TS("dve", mhT[:, c, :], mnT[:, c, :], pcol("g_mem", l, c), None,
                   ALU.mult, None, r=[mnT_b, pp_b], w=[mhT_b])
            for wi, (nm, dst, dbuf) in enumerate((("k", mk_p, None), ("v", mv_p, mv_b[l]))):
                st, stb = kvst[wi]
                for half in range(2):
                    sap, sbuf = wget(l, "%s%d" % (nm, half))
                    if cfg.get("CUT") == 4:
                        return
                    if nm == "k":
                        for m in range(4):
                            oc = half * 4 + m
                            bk = bank_lo()
                            for kc in range(8):
                                MM(ps[:, bk, 0:NMEM], sap[:, kc, m * 128:(m + 1) * 128], mhT[:, kc, :], kc == 0, kc == 7,
                                   r=[sbuf, mhT_b], w=[pb[bk]])
                            ACT(mkT[:, l, oc, :], ps[:, bk, 0:NMEM], AF.Copy, r=[pb[bk]], w=[mk_b[l]])
                        if cfg.get("CUT") == 5:
                            return
                    for mc in range(2):
                        bk = bank_lo()
                        for kc in range(8):
                            MM(ps[:, bk, :], mhT[:, kc, mc * 128:(mc + 1) * 128], sap[:, kc, :], kc == 0, kc == 7,
                               r=[sbuf, mhT_b], w=[pb[bk]])
                        CP("dve", st[:, mc, half * 512:(half + 1) * 512], ps[:, bk, :], r=[pb[bk]], w=[stb])
                        if cfg.get("CUT") == 6:
                            return
                        if nm == "v":
                            CP("pool", mvS[:, l, mc, half * 512:(half + 1) * 512],
                               st[:, mc, half * 512:(half + 1) * 512], r=[stb], w=[mv_b[l]])
                if cfg.get("CUT") == 8:
                    return
                DMA(dst[l, b].rearrange("(mc p) f -> p mc f", p=128), st, r=[stb], w=[])
                if cfg.get("CUT") == 7:
                    return
            if cfg.get("CUT") == 9:
                return

    def prep_sample(b):
        P.cur = "prep_s"
        arena_reset()
        kst, kst_b = aalloc([2, D], BF, "kst")
        cst, cst_b = aalloc([256], F32, "cst")
        pst, pst_b = aalloc([256], F32, "pst")
        for l in range(DEPTH):
            DMA(Sst[:, l], st_hg[l, b].rearrange("h d v -> d h v"), r=[], w=S_b[l])
            DMA(mvS[:, l], c_mv[l, b].rearrange("(mc p) f -> p mc f", p=128), r=[], w=[mv_b[l]], eng="pool",
                grp="cast%d" % (l % 8))
            DMA(kst, c_mk[l, b].rearrange("(mc p) f -> p mc f", p=128), r=[], w=[kst_b], eng="pool",
                grp="cast%d" % ((l + 4) % 8))
            for mc in range(2):
                bk = bank_lo()
                psb = ps[:, bk, :].bitcast(BF)
                for oc in range(8):
                    TR(psb[:, oc * 128:(oc + 1) * 128], kst[:, mc, oc * 128:(oc + 1) * 128], identb,
                       r=[kst_b, cb_b], w=[pb[bk]])
                CP("dve", mkT[:, l, :, mc * 128:(mc + 1) * 128], psb.rearrange("p (c m) -> p c m", m=128),
                   r=[pb[bk]], w=[mk_b[l]])
            for (src, n, stg, stgb, hist, hb) in ((c_conv, 30, cst, cst_b, chist, ch_b[l]),
                                                  (c_pool, 15, pst, pst_b, phist, ph_b[l])):
                DMA(stg[0:n, :], src[l, b], r=[], w=[stgb])
                bk = bank_lo()
                for ch in range(2):
                    TR(ps[:, bk, ch * 32:ch * 32 + n], stg[0:n, ch * 128:(ch + 1) * 128], identf[0:n, 0:n],
                       r=[stgb, cf_b], w=[pb[bk]])
                CP("dve", hist[:, l], ps[:, bk, 0:64].rearrange("p (c t) -> p c t", t=32)[:, :, 0:n],
                   r=[pb[bk]], w=[hb])

    def store_states(b, hg_o, conv_o, pool_o):
        P.cur = "store_st"
        arena_reset()
        cso = [aalloc([256], F32, "cso%d" % i) for i in range(2)]
        for l in range(DEPTH):
            DMA(hg_o[l, b].rearrange("h d v -> d h v"), Sst[:, l], r=S_b[l], w=[])
            for i, (dst, n, hist, hb) in enumerate(((conv_o, 30, chist, ch_b[l]), (pool_o, 15, phist, ph_b[l]))):
                so, sob = cso[i]
                bk = bank_lo()
                for ch in range(2):
                    TR(ps[0:n, bk, ch * 128:(ch + 1) * 128], hist[:, l, ch, :], identf, r=[hb, cf_b], w=[pb[bk]])
                CP("dve", so[0:n, :], ps[0:n, bk, 0:256], r=[pb[bk]], w=[sob])
                DMA(dst[l, b], so[0:n, :], r=[sob], w=[])

    STOP = cfg.get("STOP", 99)
    for kind, b in seqs:
        if STOP <= 1:
            break
        if kind == "p":
            prep_prompt(b)
            ntile, T = SEQ // TT, TT
        else:
            prep_sample(b)
            ntile, T = 1, DSEQ
        if STOP <= 2:
            break
        for t in range(ntile):
            src = x_p[b, t * TT:(t + 1) * TT, :] if kind == "p" else x_s[b]
            dst = y_p[b, t * TT:(t + 1) * TT, :] if kind == "p" else y_s[b]
            load_tile(T, src)
            nb_ = None
            if STOP > 3:
                for l in range(DEPTH):
                    nb_ = layer(T, l, kind == "p" and t == 0, nb_)
            store_tile(T, dst, nb_)
        if STOP <= 3:
            continue
        if kind == "p":
            store_states(b, hg_p, conv_p, pool_p)
        else:
            store_states(b, hg_s, conv_s, pool_s)
    assert STOP < 99 or WQ.nxt == len(plan), (WQ.nxt, len(plan))

    engs = ["pe", "act", "dve", "pool", "sp"]
    streams = {e: [] for e in engs}
    cnts = {e: 0 for e in engs}
    for o in P.ops:
        streams[o.eng].append(o)
        if o.dma is None and o.sig:
            cnts[o.eng] += 1
            o.signo = cnts[o.eng]
    import os as _os
    if _os.environ.get("DUMP_LABELS"):
        import json as _json
        _json.dump({e: [o.label for o in streams[e]] for e in engs}, open(_os.environ["DUMP_LABELS"], "w"))
    from contextlib import ExitStack
    with ExitStack() as es:
        esem = {e: es.enter_context(nc.semaphore("s_" + e)) for e in ["pe", "act", "dve", "pool"]}
        dsem = {g: es.enter_context(nc.semaphore("d_" + g)) for g in P.dma_count}
        block = es.enter_context(nc.Block())

        def emit(ename, e):
            have = {}
            for o in streams[ename]:
                for d in sorted(o.deps, key=lambda d: d.id):
                    if d.dma is not None:
                        key, val, sem = "dma:" + d.dma, 16 * d.dmaidx, dsem[d.dma]
                    else:
                        key, val, sem = d.eng, d.signo, esem[d.eng]
                    if have.get(key, 0) >= val:
                        continue
                    have[key] = val
                    e.wait_ge(sem, val)
                ins = o.fn(e)
                if o.dma is not None:
                    ins.then_inc(dsem[o.dma], 16)
                elif o.sig:
                    ins.then_inc(esem[ename], 1)
            if ename == "sp":
                for g, n in P.dma_count.items():
                    e.wait_ge(dsem[g], 16 * n)

        block.tensor(lambda e: emit("pe", e))
        block.scalar(lambda e: emit("act", e))
        block.vector(lambda e: emit("dve", e))
        block.gpsimd(lambda e: emit("pool", e))
        block.sync(lambda e: emit("sp", e))
    return nc, {e: len(streams[e]) for e in engs}


def host_params(cfg, inp):
    DEPTH = cfg["DEPTH"]
    names = [("g_mix", 8), ("g_x", 8), ("g_ffn", 8), ("g_mem", 8), ("hg_norm_g", 4), ("conv_b", 2),
             ("conv_ln_g", 2), ("conv_ln_b", 2), ("pool_scale", 2), ("conv_w", 62)]
    OFF = {}
    o = 0
    for n, k in names:
        OFF[n] = o
        o += k
    PPL = o
    base = PPL * DEPTH
    OFF["final_g"] = base
    OFF["invw"] = base + 8
    OFF["rc16"] = base + 10
    OFF["eps"] = base + 42
    OFF["lbp"] = base + 43
    NPP = base + 43 + DEPTH * 4
    cfg["OFF"], cfg["PPL"], cfg["NPP"] = OFF, PPL, NPP
    if inp is None:
        return None
    pp = np.zeros((128, NPP), np.float32)

    def fm(v, k):
        return np.ascontiguousarray(np.asarray(v, np.float32).reshape(k, 128).T)
    src = dict(g_mix="norm_mix_g", g_x="norm_x_g", g_ffn="norm_ffn_g", g_mem="norm_mem_g", hg_norm_g="hg_norm_g",
               conv_b="conv_b", conv_ln_g="conv_ln_g", conv_ln_b="conv_ln_b", pool_scale="pool_scale")
    for l in range(DEPTH):
        for n, k in names[:-1]:
            pp[:, l * PPL + OFF[n]:l * PPL + OFF[n] + k] = fm(inp[src[n]][l], k)
        cw = np.asarray(inp["conv_w"][l], np.float32)
        for ch in range(2):
            pp[:, l * PPL + OFF["conv_w"] + ch * 31:l * PPL + OFF["conv_w"] + (ch + 1) * 31] = cw[:, ch * 128:(ch + 1) * 128].T
        pp[:, OFF["lbp"] + l * 4:OFF["lbp"] + (l + 1) * 4] = fm(inp["lb_param"][l], 4)
    pp[:, OFF["final_g"]:OFF["final_g"] + 8] = fm(inp["final_g"], 8)
    pp[:, OFF["eps"]] = EPS
    wins = np.array([[2, 8], [4, 16]], np.float32)
    for ch in range(2):
        for hi in range(2):
            w = wins[hi, ch]
            sl = slice(hi * 64, (hi + 1) * 64)
            pp[sl, OFF["invw"] + ch] = 1.0 / w
            pos = np.arange(16, dtype=np.float32)
            pp[sl, OFF["rc16"] + ch * 16:OFF["rc16"] + (ch + 1) * 16] = w / np.minimum(pos + 1.0, w)
    pw = np.asarray(inp["pool_w"], np.float32)
    pwbd = np.zeros((128, DEPTH, 2, 128), np.float32)
    for l in range(DEPTH):
        for g in range(4):
            ch, hi = g // 2, g % 2
            pwbd[hi * 64:(hi + 1) * 64, l, ch, hi * 64:(hi + 1) * 64] = pw[l, g]
    cst_f = np.zeros((128, 256 + TT), np.float32)
    cst_f[:, 0:128] = np.eye(128, dtype=np.float32)
    rm = np.ones(TT, np.float32)
    rm[::64] = 0.0
    cst_f[:, 128:128 + TT] = rm[None, :]
    cst_f[:, 128 + TT:] = 1.0 / 256.0
    cst_b = np.zeros((128, 448), np.float32)
    cst_b[:, 0:128] = np.eye(128)
    cst_b[:, 128:256] = 1.0
    cst_b[:, 256:384] = 1.0 / 256.0
    s = np.arange(64)
    cst_b[0:64, 384:448] = (s[:, None] <= s[None, :]).astype(np.float32)
    return dict(pp=pp, pwbd=pwbd, cst_f=cst_f, cst_b=cst_b.astype(ml_dtypes.bfloat16))


_CACHE = {}


def run(cfg, inp):
    NP, NS, NCORES, DEPTH = cfg["NP"], cfg["NS"], cfg["NCORES"], cfg["DEPTH"]
    hp = host_params(cfg, inp)
    key = tuple(sorted((k, v) for k, v in cfg.items() if k not in ("OFF",)))
    if key not in _CACHE:
        _CACHE[key] = build(cfg)
    nc, stats = _CACHE[key]
    f = lambda a: np.ascontiguousarray(np.asarray(a, np.float32))
    shared = dict(w_in=f(inp["w_in"]), w_out=f(inp["w_out"]), xq_w=f(inp["xq_w"]), xk_w=f(inp["xk_w"]),
                  xv_w=f(inp["xv_w"]), xo_w=f(inp["xo_w"]), w_up=f(inp["w_up"]), w_down=f(inp["w_down"]),
                  pp=hp["pp"], pwbd=hp["pwbd"], cst_f=hp["cst_f"], cst_b=hp["cst_b"])
    in_maps = []
    for c in range(NCORES):
        m = dict(shared)
        m["x_prompt"] = f(inp["x_prompt"][c * NP:(c + 1) * NP])
        m["mem_prompt"] = f(inp["mem_prompt"][c * NP:(c + 1) * NP])
        m["x_sample"] = f(inp["x_sample"][c * NS:(c + 1) * NS])
        m["state_hgrn"] = f(inp["state_hgrn"][:, c * NS:(c + 1) * NS])
        m["cache_conv"] = f(inp["cache_conv"][:, c * NS:(c + 1) * NS])
        m["cache_pool"] = f(inp["cache_pool"][:, c * NS:(c + 1) * NS])
        m["cache_mem_k"] = f(inp["cache_mem_k"][:, c * NS:(c + 1) * NS]).reshape(DEPTH, NS, NMEM, D)
        m["cache_mem_v"] = f(inp["cache_mem_v"][:, c * NS:(c + 1) * NS]).reshape(DEPTH, NS, NMEM, D)
        in_maps.append(m)
    res = run_bass_kernel_spmd(nc, in_maps, core_ids=list(range(NCORES)))
    R = res.results
    cat0 = lambda k: np.concatenate([np.asarray(r[k], np.float32) for r in R], axis=0)
    cat1 = lambda k: np.concatenate([np.asarray(r[k], np.float32) for r in R], axis=1)
    BP = NP * NCORES
    return (cat0("y_prompt"), cat0("y_sample"), cat1("hg_p"), cat1("conv_p"), cat1("pool_p"),
            cat1("mk_p").reshape(DEPTH, BP, NMEM, 4, 256), cat1("mv_p").reshape(DEPTH, BP, NMEM, 4, 256),
            cat1("hg_s"), cat1("conv_s"), cat1("pool_s"))


def kernel(**inputs):
    cfg = dict(CFG)
    return run(cfg, inputs)
```
